# Optimizing a Trainium2 kernel written in Bass

```python
import math
import jax
import jax.numpy as jnp
from jax import lax
import numpy as np

D_MODEL = 4096
BATCH = 4
SEQ = 4096
DEPTH = 4

N_MIXERS = 4
N_HEADS = 32
HEAD_DIM = D_MODEL // N_HEADS
D_FF = 4 * D_MODEL
ROPE_THETA = 10000.0
DILATED_PATTERNS = ((128, 1), (512, 4), (2048, 16))
Q_BLOCK = 128
POOL_WINDOWS = (2, 4, 8, 16)
POOL_GROUP = D_MODEL // len(POOL_WINDOWS)
CONV_WIDTH = 3
LN_EPS = 1e-5
DEEPNORM_ALPHA = (2.0 * DEPTH) ** 0.25
DEEPNORM_BETA = (8.0 * DEPTH) ** -0.25

kernel_name = 'hybrid_interleaved_dilated_pool_fox_shortconv'


def _n_layers_of(kind):
    return len(range(kind, DEPTH, N_MIXERS))


def layer_norm(x, g, b):
    xf = x.astype(jnp.float32)
    mu = jnp.mean(xf, axis=-1, keepdims=True)
    xc = xf - mu
    var = jnp.mean(xc * xc, axis=-1, keepdims=True)
    y = xc * lax.rsqrt(var + LN_EPS) * g.astype(jnp.float32) + b.astype(jnp.float32)
    return y.astype(x.dtype)


def rope_tables(S):
    pos = jnp.arange(S, dtype=jnp.float32)
    inv = ROPE_THETA ** (-jnp.arange(0, HEAD_DIM, 2, dtype=jnp.float32) / HEAD_DIM)
    ang = pos[:, None] * inv[None, :]
    ang = jnp.concatenate([ang, ang], axis=-1)
    return jnp.cos(ang), jnp.sin(ang)


def apply_rope(t, cos, sin):
    tf = t.astype(jnp.float32)
    t1, t2 = jnp.split(tf, 2, axis=-1)
    rot = jnp.concatenate([-t2, t1], axis=-1)
    return (tf * cos[:, None, :] + rot * sin[:, None, :]).astype(t.dtype)


def dilated_branch(q, k, v, window, dilation):
    B, S, H, Dh = q.shape
    blk = window // dilation
    chunk = blk * dilation
    L = -(-S // chunk) * chunk
    nb = L // chunk

    def to_classes(t):
        t = jnp.pad(t, ((0, 0), (0, L - S), (0, 0), (0, 0)))
        t = t.reshape(B, nb * blk, dilation, H, Dh).transpose(0, 2, 1, 3, 4)
        return t.reshape(B, dilation, nb, blk, H, Dh)

    def with_prev(t):
        prev = jnp.pad(t[:, :, :-1], ((0, 0), (0, 0), (1, 0), (0, 0), (0, 0), (0, 0)))
        return jnp.concatenate([prev, t], axis=3)

    qs, ks, vs = to_classes(q), to_classes(k), to_classes(v)
    kk, vv = with_prev(ks), with_prev(vs)
    s = jnp.einsum('brnqhd,brnkhd->brnhqk', qs, kk).astype(jnp.float32)
    qi = jnp.arange(blk)[:, None]
    ki = jnp.arange(2 * blk)[None, :]
    rel = blk + qi - ki
    band = (rel >= 0) & (rel <= blk)
    has_prev = (jnp.arange(nb) > 0)[:, None, None] | (ki >= blk)[None]
    valid = band[None] & has_prev
    s = jnp.where(valid[None, None, :, None], s, -jnp.inf)
    m = jnp.max(s, axis=-1, keepdims=True)
    p = jnp.exp(s - m)
    den = jnp.sum(p, axis=-1)
    o = jnp.einsum('brnhqk,brnkhd->brnqhd', p, vv.astype(jnp.float32))
    o = o / jnp.swapaxes(den, -1, -2)[..., None]
    lse = jnp.swapaxes(m[..., 0] + jnp.log(den), -1, -2)

    def from_classes(t):
        t = t.reshape((B, dilation, nb * blk) + t.shape[4:])
        t = jnp.moveaxis(t, 1, 2).reshape((B, L) + t.shape[3:])
        return t[:, :S]

    return from_classes(o), from_classes(lse)


def mixer_dilated(x, wqkv, wo, cos, sin):
    B, S, D = x.shape
    q, k, v = jnp.split(x @ wqkv, 3, axis=-1)
    q = apply_rope(q.reshape(B, S, N_HEADS, HEAD_DIM), cos, sin) * (HEAD_DIM ** -0.5)
    k = apply_rope(k.reshape(B, S, N_HEADS, HEAD_DIM), cos, sin)
    v = v.reshape(B, S, N_HEADS, HEAD_DIM)
    outs, lses = [], []
    for window, dilation in DILATED_PATTERNS:
        o, l = dilated_branch(q, k, v, window, dilation)
        outs.append(o)
        lses.append(l)
    wts = jax.nn.softmax(jnp.stack(lses, axis=0), axis=0)
    o = jnp.sum(wts[..., None] * jnp.stack(outs, axis=0), axis=0)
    return o.reshape(B, S, D).astype(x.dtype) @ wo


def mixer_pool(x, wgrp, scale):
    B, S, D = x.shape
    groups = x.astype(jnp.float32).reshape(B, S, len(POOL_WINDOWS), POOL_GROUP)
    t = jnp.arange(1, S + 1, dtype=jnp.float32)
    pooled = []
    for g, w in enumerate(POOL_WINDOWS):
        xg = groups[:, :, g]
        cs = jnp.pad(jnp.cumsum(xg, axis=1), ((0, 0), (1, 0), (0, 0)))
        hi = cs[:, 1:]
        lo = jnp.pad(cs, ((0, 0), (w - 1, 0), (0, 0)))[:, :S]
        cnt = jnp.minimum(t, float(w))[None, :, None]
        pooled.append((hi - lo) / cnt - xg)
    pooled = jnp.stack(pooled, axis=2).astype(x.dtype)
    y = jnp.einsum('bsgc,gcd->bsgd', pooled, wgrp).reshape(B, S, D)
    return y * scale


def mixer_fox(x, win, bf, wo):
    B, S, D = x.shape
    proj = x @ win
    q = proj[..., :D].reshape(B, S, N_HEADS, HEAD_DIM) * (HEAD_DIM ** -0.5)
    k = proj[..., D:2 * D].reshape(B, S, N_HEADS, HEAD_DIM)
    v = proj[..., 2 * D:3 * D].reshape(B, S, N_HEADS, HEAD_DIM)
    logf = jax.nn.log_sigmoid(proj[..., 3 * D:].astype(jnp.float32) + bf.astype(jnp.float32))
    c = jnp.swapaxes(jnp.cumsum(logf, axis=1), 1, 2)
    nq = S // Q_BLOCK
    qb = q.reshape(B, nq, Q_BLOCK, N_HEADS, HEAD_DIM).transpose(1, 0, 2, 3, 4)
    cb = c.reshape(B, N_HEADS, nq, Q_BLOCK).transpose(2, 0, 1, 3)
    kpos = jnp.arange(S)

    def block(args):
        q_i, c_i, i = args
        s = jnp.einsum('bqhd,bkhd->bhqk', q_i, k).astype(jnp.float32)
        s = s + (c_i[..., :, None] - c[:, :, None, :])
        qpos = i * Q_BLOCK + jnp.arange(Q_BLOCK)
        s = jnp.where(kpos[None, :] <= qpos[:, None], s, -jnp.inf)
        p = jax.nn.softmax(s, axis=-1)
        return jnp.einsum('bhqk,bkhd->bqhd', p.astype(v.dtype), v)

    o = lax.map(block, (qb, cb, jnp.arange(nq)))
    o = o.transpose(1, 0, 2, 3, 4).reshape(B, S, D)
    return o @ wo


def mixer_conv(x, win, wconv, wout):
    D = x.shape[-1]
    b_gate, c_gate, h = jnp.split(x @ win, 3, axis=-1)
    z = c_gate * h
    conv = lax.conv_general_dilated(
        z, wconv[:, None, :].astype(z.dtype), window_strides=(1,),
        padding=[(CONV_WIDTH - 1, 0)], dimension_numbers=('NWC', 'WIO', 'NWC'),
        feature_group_count=D)
    return (b_gate * conv) @ wout


def mlp_sqrelu(x, w1, w2):
    h = jax.nn.relu(x @ w1)
    return (h * h) @ w2


def setup_inputs(seed: int = 0) -> dict:
    key = jax.random.key(seed)
    ks = jax.random.split(key, 20)
    D = D_MODEL
    f32 = jnp.float32

    def nrm(k, shape, scale):
        return jax.random.normal(k, shape, f32) * scale

    nA, nB, nC, nD = (_n_layers_of(i) for i in range(N_MIXERS))
    v_cols = jnp.concatenate([jnp.ones((2 * D,), f32), jnp.full((D,), DEEPNORM_BETA, f32)])
    fox_cols = jnp.concatenate([v_cols, jnp.ones((N_HEADS,), f32)])
    return {
        'x': nrm(ks[0], (BATCH, SEQ, D), 1.0),
        'ln_g': 1.0 + nrm(ks[1], (DEPTH, 2, D), 0.02),
        'ln_b': nrm(ks[2], (DEPTH, 2, D), 0.02),
        'mlp_w1': nrm(ks[3], (DEPTH, D, D_FF), D ** -0.5),
        'mlp_w2': nrm(ks[4], (DEPTH, D_FF, D), D_FF ** -0.5 * DEEPNORM_BETA),
        'a_wqkv': nrm(ks[5], (nA, D, 3 * D), D ** -0.5) * v_cols,
        'a_wo': nrm(ks[6], (nA, D, D), D ** -0.5 * DEEPNORM_BETA),
        'b_wgrp': nrm(ks[7], (nB, len(POOL_WINDOWS), POOL_GROUP, POOL_GROUP), POOL_GROUP ** -0.5 * DEEPNORM_BETA),
        'b_scale': 1.0 + nrm(ks[8], (nB, D), 0.02),
        'c_win': nrm(ks[9], (nC, D, 3 * D + N_HEADS), D ** -0.5) * fox_cols,
        'c_bf': jax.random.uniform(ks[10], (nC, N_HEADS), f32, 1.0, 5.0),
        'c_wo': nrm(ks[11], (nC, D, D), D ** -0.5 * DEEPNORM_BETA),
        'd_win': nrm(ks[12], (nD, D, 3 * D), D ** -0.5),
        'd_conv': nrm(ks[13], (nD, CONV_WIDTH, D), CONV_WIDTH ** -0.5),
        'd_wout': nrm(ks[14], (nD, D, D), D ** -0.5 * DEEPNORM_BETA),
    }


def reference(x, ln_g, ln_b, mlp_w1, mlp_w2, a_wqkv, a_wo, b_wgrp, b_scale,
              c_win, c_bf, c_wo, d_win, d_conv, d_wout):
    S = x.shape[1]
    cos, sin = rope_tables(S)
    for i in range(DEPTH):
        kind, j = i % N_MIXERS, i // N_MIXERS
        if kind == 0:
            y = mixer_dilated(x, a_wqkv[j], a_wo[j], cos, sin)
        elif kind == 1:
            y = mixer_pool(x, b_wgrp[j], b_scale[j])
        elif kind == 2:
            y = mixer_fox(x, c_win[j], c_bf[j], c_wo[j])
        else:
            y = mixer_conv(x, d_win[j], d_conv[j], d_wout[j])
        x = layer_norm(DEEPNORM_ALPHA * x + y, ln_g[i, 0], ln_b[i, 0])
        x = layer_norm(DEEPNORM_ALPHA * x + mlp_sqrelu(x, mlp_w1[i], mlp_w2[i]), ln_g[i, 1], ln_b[i, 1])
    return x
```

```python
import numpy as np
import ml_dtypes
from contextlib import ExitStack
import concourse.bass as bass
import concourse.mybir as mybir
from concourse.bass_utils import run_bass_kernel_spmd

F32 = mybir.dt.float32
BF16 = mybir.dt.bfloat16
AF = mybir.ActivationFunctionType
ALU = mybir.AluOpType
ENG = ['pe', 'act', 'dve', 'pool', 'sp']
LN_EPS = 1e-5
NEG = -30000.0


class Sem:
    def __init__(self, nc, name):
        self.h = nc.alloc_semaphore(name)
        self.n = 0


class Prog:
    def __init__(self, nc):
        self.nc = nc
        self.q = {e: [] for e in ENG}
        self.sems = {}
        self.pending = {}

    def S(self, name):
        if name not in self.sems:
            self.sems[name] = Sem(self.nc, name)
        return self.sems[name]

    def op(self, eng, meth, *a, inc=None, **kw):
        n = 0
        h = None
        t = None
        if eng in ('act', 'dve', 'pool'):
            inc = self.S(eng)
            if inc.n > 0:
                self.q[eng].append(('wait_ge', (inc.h, inc.n), {}, None, 0))
        if inc is not None:
            inc.n += 1
            h, n, t = inc.h, 1, (inc, inc.n)
        self.q[eng].append((meth, a, kw, h, n))
        return t

    def dma(self, eng, out, in_, sem, output=False):
        s = self.S(sem) if isinstance(sem, str) else sem
        s.n += 16
        self.q[eng].append(('dma_start', (), dict(out=out, in_=in_), s.h, 16))
        t = (s, s.n)
        if output:
            self.pending[s] = s.n
        return t

    def wait(self, eng, t):
        if t is None:
            return
        s, v = t
        if v <= 0:
            return
        self.q[eng].append(('wait_ge', (s.h, v), {}, None, 0))

    def phase_end(self):
        for s, v in self.pending.items():
            for e in ENG:
                self.wait(e, (s, v))
        self.pending = {}

    def run(self):
        nc = self.nc
        with nc.Block() as block:
            def mk(name):
                def f(e):
                    for meth, a, kw, h, n in self.q[name]:
                        ins = getattr(e, meth)(*a, **kw)
                        if h is not None:
                            ins.then_inc(h, n)
                return f
            block.tensor(mk('pe'))
            block.scalar(mk('act'))
            block.vector(mk('dve'))
            block.gpsimd(mk('pool'))
            block.sync(mk('sp'))


class Caster:
    NSEM = 8

    def __init__(self, P):
        self.P = P
        self.jobs = []
        self.next = 0
        self.tick = {}

    def add(self, region, dst, src, rows, cols):
        for r0 in range(0, rows, 128):
            for c0 in range(0, cols, 4096):
                c1 = min(cols, c0 + 4096)
                self.jobs.append((region, dst[r0:r0 + 128, c0:c1], src[r0:r0 + 128, c0:c1]))

    def emit(self, n, window=None):
        P = self.P
        while n > 0 and self.next < len(self.jobs):
            _, dst, src = self.jobs[self.next]
            s = P.S(f'cast{self.next % self.NSEM}')
            P.wait('pool', (s, s.n))
            if window is not None and self.next - window >= 0:
                P.wait('pool', self.tick.get(self.next - window))
            self.tick[self.next] = P.dma('pool', dst, src, s)
            self.next += 1
            n -= 1

    def ensure(self, region):
        P = self.P
        last = -1
        for i, j in enumerate(self.jobs):
            if j[0] <= region:
                last = i
        while self.next <= last:
            self.emit(1)
        for i in range(max(0, last - self.NSEM + 1), last + 1):
            P.wait('pool', self.tick[i])


def bcast_ap(handle, offset, n):
    return bass.AP(handle, offset, [[0, 128], [1, n]])


class Ctx:
    pass


def emit_transpose_store(P, C, rowb, t_rowb, XT, tt, stg, tps, k):
    DC = C.DC
    sb = stg[k % 2]
    s_st = P.S(f'ts_st{k % 2}')
    P.wait('act', (s_st, s_st.n))
    P.wait('pe', t_rowb)
    ng = (DC + 7) // 8
    t_last_pe = None
    for g in range(ng):
        gi = C.tp_cnt
        C.tp_cnt += 1
        pb = tps[gi % 2]
        P.wait('pe', C.tp_free[gi % 2])
        n = min(8, DC - g * 8)
        for j in range(n):
            c = g * 8 + j
            t = P.op('pe', 'transpose', out=pb[:, j * 128:(j + 1) * 128], in_=rowb[:, c * 128:(c + 1) * 128],
                     identity=C.ident[:], inc=(P.S('pe') if j == n - 1 else None))
        t_last_pe = t
        P.wait('act', t)
        tf = P.op('act', 'activation', out=sb[:, g * 8:g * 8 + n, :], in_=pb[:, 0:n * 128].rearrange("p (a b) -> p a b", b=128),
                  func=AF.Copy, inc=P.S('act'))
        C.tp_free[gi % 2] = tf
    P.wait('sp', tf)
    P.dma('sp', XT[:, :, tt * 128:(tt + 1) * 128].rearrange("c p t -> p c t"), sb[:], s_st, output=True)
    return t_last_pe


def phase_prep(P, C, X, OUT, XT):
    nc = C.nc
    D, S, DC = C.D, C.S, C.DC
    with ExitStack() as es:
        rows = [es.enter_context(nc.sbuf_tensor(f"pr_row{i}", [128, D], F32)) for i in range(2)]
        rowb = [es.enter_context(nc.sbuf_tensor(f"pr_rowb{i}", [128, D], BF16)) for i in range(2)]
        stg = [es.enter_context(nc.sbuf_tensor(f"pr_stg{i}", [128, DC, 128], BF16)) for i in range(2)]
        tps = [es.enter_context(nc.psum_tensor(f"pr_tps{i}", [128, 1024], BF16)) for i in range(2)]
        C.tp_cnt = 0
        C.tp_free = [None, None]
        free_row = [None, None]
        free_rowb = [None, None]
        for tt in range(S // 128):
            b = tt % 2
            P.wait('sp', free_row[b])
            t_ld = P.dma('sp', rows[b][:], X[tt * 128:(tt + 1) * 128, :], f'pr_ld{b}')
            P.wait('sp', t_ld)
            P.dma('sp', OUT[tt * 128:(tt + 1) * 128, :], rows[b][:], f'pr_so{b}', output=True)
            P.wait('dve', t_ld)
            P.wait('dve', free_rowb[b])
            t_c = P.op('dve', 'tensor_copy', out=rowb[b][:], in_=rows[b][:], inc=P.S('dve'))
            sso = P.S(f'pr_so{b}')
            free_row[b] = (sso, sso.n)
            free_rowb[b] = emit_transpose_store(P, C, rowb[b], t_c, XT, tt, stg, tps, tt)
        P.phase_end()


def phase_lin_f(P, C, XTin, KC, W, col_chunks, epi, name, M=128, TB=1024, region=None, xbufs=2):
    nc = C.nc
    S = C.S
    TB = min(TB, S)
    NTB = S // TB
    NH = TB // 512
    Wv = W.rearrange("(kc p) f -> p kc f", p=128)
    groups = []
    i = 0
    while i < len(col_chunks):
        if M == 128 and i + 1 < len(col_chunks) and col_chunks[i + 1] == col_chunks[i] + 128:
            groups.append((col_chunks[i], 2, i))
            i += 2
        else:
            groups.append((col_chunks[i], 1, i))
            i += 1
    if region is not None:
        C.caster.ensure(region)
    with ExitStack() as es:
        xbs = [es.enter_context(nc.sbuf_tensor(f"{name}_xb{i}", [128, KC, TB], BF16)) for i in range(xbufs)]
        NW = 3
        wt = [es.enter_context(nc.sbuf_tensor(f"{name}_wt{i}", [128, KC, 256], BF16)) for i in range(NW)]
        NPS = 4
        ps = [es.enter_context(nc.psum_tensor(f"{name}_ps{i}", [128, 512], F32)) for i in range(NPS)]
        C.NH = NH
        st = epi('alloc', es)
        ps_free = [None] * NPS
        w_free = [None] * NW
        wi = 0
        pi = 0
        x_done = [None] * xbufs
        x_tick = {}

        def xload(tb):
            bi = tb % xbufs
            P.wait('sp', x_done[bi])
            x_tick[tb] = P.dma('sp', xbs[bi][:], XTin[:, :, tb * TB:(tb + 1) * TB].rearrange("k p t -> p k t"), f'g_x{bi}')
        if xbufs > 1:
            xload(0)
        for tb in range(NTB):
            if xbufs > 1:
                if tb + 1 < NTB:
                    xload(tb + 1)
            else:
                xload(tb)
            xb = xbs[tb % xbufs]
            epi('block', tb, st)
            P.wait('pe', x_tick[tb])
            for (c0, nch, ci0) in groups:
                b = wi % NW
                wi += 1
                C.caster.emit(1)
                P.wait('pool', w_free[b])
                t_w = P.dma('pool', wt[b][:, :, 0:nch * M], Wv[:, :, c0:c0 + nch * M], f'g_w{b}')
                P.wait('pe', t_w)
                for m in range(nch):
                    for hf in range(NH):
                        p = pi % NPS
                        pi += 1
                        P.wait('pe', ps_free[p])
                        for k in range(KC):
                            t = P.op('pe', 'matmul', ps[p][0:M, :], lhsT=wt[b][:, k, m * M:(m + 1) * M],
                                     rhs=xb[:, k, hf * 512:(hf + 1) * 512],
                                     start=(k == 0), stop=(k == KC - 1), inc=(P.S('pe') if k == KC - 1 else None))
                        ps_free[p] = epi('chunk', tb * NH + hf, ci0 + m, ps[p], t, st)
                w_free[b] = t
            x_done[tb % xbufs] = t
        epi('end', st)
        P.phase_end()


def epi_copy_scale(P, C, OUTT, scale, name):
    nc = C.nc

    def epi(kind, *a):
        if kind == 'alloc':
            es = a[0]
            st = Ctx()
            st.ob = [es.enter_context(nc.sbuf_tensor(f"{name}_ob{i}", [128, 512], BF16)) for i in range(3)]
            st.k = 0
            return st
        if kind in ('block', 'end'):
            return None
        tb, ci, ps, t_ready, st = a
        b = st.k % 3
        st.k += 1
        so = P.S(f'g_o{b}')
        P.wait('act', t_ready)
        P.wait('act', (so, so.n))
        t = P.op('act', 'activation', out=st.ob[b][:], in_=ps[:], func=AF.Copy, scale=float(scale), inc=P.S('act'))
        P.wait('sp', t)
        P.dma('sp', OUTT[ci, :, tb * 512:(tb + 1) * 512], st.ob[b][:], so, output=True)
        return t
    return epi


def epi_relu2(P, C, OUTT, name):
    nc = C.nc

    def epi(kind, *a):
        if kind == 'alloc':
            es = a[0]
            st = Ctx()
            st.r = [es.enter_context(nc.sbuf_tensor(f"{name}_r{i}", [128, 512], F32)) for i in range(2)]
            st.ob = [es.enter_context(nc.sbuf_tensor(f"{name}_ob{i}", [128, 512], BF16)) for i in range(3)]
            st.k = 0
            st.rfree = [None, None]
            return st
        if kind in ('block', 'end'):
            return None
        tb, ci, ps, t_ready, st = a
        b = st.k % 3
        rb = st.k % 2
        st.k += 1
        so = P.S(f'g_o{b}')
        P.wait('act', t_ready)
        P.wait('act', st.rfree[rb])
        t = P.op('act', 'activation', out=st.r[rb][:], in_=ps[:], func=AF.Relu, inc=P.S('act'))
        P.wait('dve', t)
        P.wait('dve', (so, so.n))
        t2 = P.op('dve', 'tensor_tensor', out=st.ob[b][:], in0=st.r[rb][:], in1=st.r[rb][:], op=ALU.mult, inc=P.S('dve'))
        st.rfree[rb] = t2
        P.wait('sp', t2)
        P.dma('sp', OUTT[ci, :, tb * 512:(tb + 1) * 512], st.ob[b][:], so, output=True)
        return t
    return epi


def phase_lin_t(P, C, XTin, KC, W, ncb, kmap, epi, name, TB, region=None, NPS=6):
    nc = C.nc
    S = C.S
    NTB = S // TB
    NT = TB // 128
    Wv = W.rearrange("(kc p) f -> p kc f", p=128)
    KG = 16
    if region is not None:
        C.caster.ensure(region)
    with ExitStack() as es:
        xb = es.enter_context(nc.sbuf_tensor(f"{name}_xb", [128, KC, TB], BF16))
        NW = 2
        wt = [es.enter_context(nc.sbuf_tensor(f"{name}_wt{i}", [128, KG, 512], BF16)) for i in range(NW)]
        ps = [es.enter_context(nc.psum_tensor(f"{name}_ps{i}", [128, 512], F32)) for i in range(NPS)]
        st = epi('alloc', es)
        ps_free = [None] * NPS
        w_free = [None] * NW
        wi = 0
        pi = 0
        t_xfree = None
        for tb in range(NTB):
            P.wait('sp', t_xfree)
            t_x = P.dma('sp', xb[:], XTin[:, :, tb * TB:(tb + 1) * TB].rearrange("k p t -> p k t"), 'g_x')
            epi('block', tb, st)
            P.wait('pe', t_x)
            for cb in range(ncb):
                xks, wks, wc0 = kmap(cb)
                nk = len(xks)
                ngr = (nk + KG - 1) // KG
                pbase = pi
                pi += NT
                for g in range(ngr):
                    b = wi % NW
                    wi += 1
                    k0 = g * KG
                    n = min(KG, nk - k0)
                    C.caster.emit(1)
                    P.wait('pool', w_free[b])
                    t_w = P.dma('pool', wt[b][:, 0:n, :], Wv[:, wks[k0]:wks[k0] + n, wc0:wc0 + 512], f'g_w{b}')
                    P.wait('pe', t_w)
                    for tt in range(NT):
                        p = (pbase + tt) % NPS
                        if g == 0:
                            P.wait('pe', ps_free[p])
                        for k in range(n):
                            kk = k0 + k
                            last = (kk == nk - 1)
                            t = P.op('pe', 'matmul', ps[p][:], lhsT=xb[:, xks[kk], tt * 128:(tt + 1) * 128], rhs=wt[b][:, k, :],
                                     start=(kk == 0), stop=last,
                                     inc=(P.S('pe') if (k == n - 1) else None))
                        if g == ngr - 1:
                            ps_free[p] = epi('tile', tb, tt, cb, ps[p], t, st)
                    w_free[b] = t
            t_xfree = t
            epi('blockend', tb, st)
        epi('end', st)
        P.phase_end()


def phase_lin_t_ks(P, C, XTin, KC, W, ncb, epi, name, TB, region=None, NPS=6):
    nc = C.nc
    S = C.S
    NTB = S // TB
    NT = TB // 128
    PK = 16
    nparts = KC // PK
    Wv = W.rearrange("(kc p) f -> p kc f", p=128)
    if region is not None:
        C.caster.ensure(region)
    with ExitStack() as es:
        xb = [es.enter_context(nc.sbuf_tensor(f"{name}_xb{i}", [128, PK, TB], BF16)) for i in range(2)]
        NW = 3
        wt = [es.enter_context(nc.sbuf_tensor(f"{name}_wt{i}", [128, PK, 512], BF16)) for i in range(NW)]
        ps = [es.enter_context(nc.psum_tensor(f"{name}_ps{i}", [128, 512], F32)) for i in range(NPS)]
        st = epi('alloc', es)
        ps_free = [None] * NPS
        w_free = [None] * NW
        wi = 0
        pi = 0
        seq = [(tb, part) for tb in range(NTB) for part in range(nparts)]
        x_done = [None, None]
        x_tick = {}

        def load(idx):
            tb, part = seq[idx]
            bi = idx % 2
            P.wait('sp', x_done[bi])
            x_tick[idx] = P.dma('sp', xb[bi][:], XTin[part * PK:(part + 1) * PK, :, tb * TB:(tb + 1) * TB].rearrange("k p t -> p k t"),
                                f'g_xk{bi}')
        load(0)
        for idx, (tb, part) in enumerate(seq):
            bi = idx % 2
            if idx + 1 < len(seq):
                load(idx + 1)
            if part == 0:
                epi('block', tb, st)
            P.wait('pe', x_tick[idx])
            for cb in range(ncb):
                b = wi % NW
                wi += 1
                if wi % 2 == 0:
                    C.caster.emit(1)
                P.wait('pool', w_free[b])
                t_w = P.dma('pool', wt[b][:], Wv[:, part * PK:(part + 1) * PK, cb * 512:(cb + 1) * 512], f'g_w{b}')
                P.wait('pe', t_w)
                for tt in range(NT):
                    p = pi % NPS
                    pi += 1
                    P.wait('pe', ps_free[p])
                    for k in range(PK):
                        t = P.op('pe', 'matmul', ps[p][:], lhsT=xb[bi][:, k, tt * 128:(tt + 1) * 128], rhs=wt[b][:, k, :],
                                 start=(k == 0), stop=(k == PK - 1), inc=(P.S('pe') if k == PK - 1 else None))
                    ps_free[p] = epi('tile', tb, tt, cb, ps[p], t, st, part == 0, part == nparts - 1)
                w_free[b] = t
            x_done[bi] = t
            if part == nparts - 1:
                epi('blockend', tb, st)
        epi('end', st)
        P.phase_end()


def epi_v(P, C, VV, name):
    nc = C.nc

    def epi(kind, *a):
        if kind == 'alloc':
            es = a[0]
            st = Ctx()
            st.ob = [es.enter_context(nc.sbuf_tensor(f"{name}_ob{i}", [128, 512], BF16)) for i in range(3)]
            st.k = 0
            return st
        if kind != 'tile':
            return None
        tb, tt, cb, ps, t_ready, st = a
        b = st.k % 3
        st.k += 1
        so = P.S(f'g_o{b}')
        P.wait('act', t_ready)
        P.wait('act', (so, so.n))
        t = P.op('act', 'activation', out=st.ob[b][:], in_=ps[:], func=AF.Copy, inc=P.S('act'))
        P.wait('sp', t)
        r0 = tb * 512 + tt * 128
        P.dma('sp', VV[r0:r0 + 128, cb * 512:(cb + 1) * 512], st.ob[b][:], so, output=True)
        return t
    return epi


def epi_ln(P, C, OUT, XT, g_off, b_off, name, TB, scale_off=None):
    nc = C.nc
    D, DC = C.D, C.DC
    NT = TB // 128
    ncb = D // 512

    def epi(kind, *a):
        if kind == 'alloc':
            es = a[0]
            st = Ctx()
            st.rows = [es.enter_context(nc.sbuf_tensor(f"{name}_row{i}", [128, D], F32)) for i in range(NT)]
            st.rowb = es.enter_context(nc.sbuf_tensor(f"{name}_rowb", [128, D], BF16))
            st.gB = es.enter_context(nc.sbuf_tensor(f"{name}_gB", [128, D], F32))
            st.bB = es.enter_context(nc.sbuf_tensor(f"{name}_bB", [128, D], F32))
            st.stat = es.enter_context(nc.sbuf_tensor(f"{name}_stat", [128, 8], F32))
            st.stg = [es.enter_context(nc.sbuf_tensor(f"{name}_stg{i}", [128, DC, 128], BF16)) for i in range(2)]
            st.tps = [es.enter_context(nc.psum_tensor(f"{name}_tps{i}", [128, 1024], BF16)) for i in range(2)]
            st.bst = es.enter_context(nc.sbuf_tensor(f"{name}_bst", [128, D // 512, 6], F32))
            C.tp_cnt = 0
            C.tp_free = [None, None]
            st.tk = 0
            t1 = P.dma('sp', st.gB[:], bcast_ap(C.h_lng, g_off, D), 'g_gb')
            t2 = P.dma('sp', st.bB[:], bcast_ap(C.h_lnb, b_off, D), 'g_gb')
            st.t_gb = t2
            if scale_off is not None:
                st.sB = es.enter_context(nc.sbuf_tensor(f"{name}_sB", [128, D], F32))
                st.tmp = es.enter_context(nc.sbuf_tensor(f"{name}_tmp", [128, 512], F32))
                st.t_gb = P.dma('sp', st.sB[:], bcast_ap(C.h_bscale, scale_off, D), 'g_gb')
            P.wait('dve', st.t_gb)
            st.row_free = [None] * NT
            st.rowb_free = None
            st.t_rows = [None] * NT
            return st
        if kind == 'block':
            tb, st = a
            for tt in range(NT):
                if st.row_free[tt] is not None:
                    P.wait('sp', st.row_free[tt][0])
                    P.wait('sp', st.row_free[tt][1])
                r0 = tb * TB + tt * 128
                st.t_rows[tt] = P.dma('sp', st.rows[tt][:], OUT[r0:r0 + 128, :], f'g_ldrow{tt}')
            return None
        if kind == 'tile':
            if len(a) == 8:
                tb, tt, cb, ps, t_ready, st, kfirst, klast = a
            else:
                tb, tt, cb, ps, t_ready, st = a
                kfirst, klast = True, True
            row = st.rows[tt]
            P.wait('dve', t_ready)
            if cb == 0 and kfirst:
                P.wait('dve', st.t_rows[tt])
            sl = slice(cb * 512, (cb + 1) * 512)
            if not kfirst:
                t = P.op('dve', 'tensor_tensor', out=row[:, sl], in0=row[:, sl], in1=ps[:], op=ALU.add, inc=P.S('dve'))
            elif scale_off is not None:
                P.op('dve', 'tensor_tensor', out=st.tmp[:], in0=ps[:], in1=st.sB[:, sl], op=ALU.mult)
                t = P.op('dve', 'scalar_tensor_tensor', out=row[:, sl], in0=row[:, sl], scalar=float(C.alpha), in1=st.tmp[:],
                         op0=ALU.mult, op1=ALU.add, inc=P.S('dve'))
            else:
                t = P.op('dve', 'scalar_tensor_tensor', out=row[:, sl], in0=row[:, sl], scalar=float(C.alpha), in1=ps[:],
                         op0=ALU.mult, op1=ALU.add, inc=P.S('dve'))
            if cb == ncb - 1 and klast:
                stat = st.stat
                for c8 in range(D // 512):
                    P.op('dve', 'bn_stats', out=st.bst[:, c8, :], in_=row[:, c8 * 512:(c8 + 1) * 512])
                P.op('dve', 'bn_aggr', out=stat[:, 2:4], in_=st.bst[:].rearrange("p a b -> p (a b)"))
                tv = P.op('dve', 'tensor_scalar', out=stat[:, 4:5], in0=stat[:, 3:4], scalar1=float(LN_EPS), scalar2=None,
                          op0=ALU.add)
                P.wait('act', tv)
                tq = P.op('act', 'activation', out=stat[:, 4:5], in_=stat[:, 4:5], func=AF.Sqrt)
                P.wait('dve', tq)
                P.op('dve', 'reciprocal', out=stat[:, 5:6], in_=stat[:, 4:5])
                P.op('dve', 'tensor_tensor', out=stat[:, 6:7], in0=stat[:, 2:3], in1=stat[:, 5:6], op=ALU.mult)
                tb_ = P.op('dve', 'tensor_scalar', out=stat[:, 6:7], in0=stat[:, 6:7], scalar1=-1.0, scalar2=None, op0=ALU.mult,
                           inc=P.S('dve'))
                P.wait('act', tb_)
                tn = P.op('act', 'activation', out=row[:], in_=row[:], func=AF.Identity, bias=stat[:, 6:7], scale=stat[:, 5:6],
                          inc=P.S('act'))
                P.wait('dve', tn)
                P.op('dve', 'tensor_tensor', out=row[:], in0=row[:], in1=st.gB[:], op=ALU.mult)
                tg = P.op('dve', 'tensor_tensor', out=row[:], in0=row[:], in1=st.bB[:], op=ALU.add, inc=P.S('dve'))
                P.wait('sp', tg)
                r0 = tb * TB + tt * 128
                tso = P.dma('sp', OUT[r0:r0 + 128, :], row[:], f'g_strow{tt}', output=True)
                P.wait('act', tg)
                P.wait('act', st.rowb_free)
                tc_ = P.op('act', 'activation', out=st.rowb[:], in_=row[:], func=AF.Copy, inc=P.S('act'))
                st.rowb_free = emit_transpose_store(P, C, st.rowb, tc_, XT, (tb * TB) // 128 + tt, st.stg, st.tps, st.tk)
                st.tk += 1
                st.row_free[tt] = (tso, tc_)
            return t
        return None
    return epi


def phase_attn(P, C, QT, KT, VV, OT, kind, name, QB=None, KB=None):
    nc = C.nc
    S, H = C.S, C.H
    NQB = S // 512
    NKT = S // 128
    dil = (kind == 'dil')
    VVv = VV.rearrange("(n p) d -> p n d", p=128)
    with ExitStack() as es:
        nkq = 1 if dil else 2
        kt_l = [es.enter_context(nc.sbuf_tensor(f"{name}_kt{i}", [128, S], BF16)) for i in range(nkq)]
        qt_l = [es.enter_context(nc.sbuf_tensor(f"{name}_qt{i}", [128, S], BF16)) for i in range(nkq)]
        v_l = [es.enter_context(nc.sbuf_tensor(f"{name}_v{i}", [128, NKT, 128], BF16)) for i in range(2)]
        pt = [es.enter_context(nc.sbuf_tensor(f"{name}_pt{i}", [128, 512], BF16)) for i in range(4)]
        ot = [es.enter_context(nc.sbuf_tensor(f"{name}_ot{i}", [128, 512], BF16)) for i in range(2)]
        rden = es.enter_context(nc.sbuf_tensor(f"{name}_rden", [128, 512], F32))
        ones = es.enter_context(nc.sbuf_tensor(f"{name}_ones", [128, 128], BF16))
        ps_s = [es.enter_context(nc.psum_tensor(f"{name}_pss{i}", [128, 512], F32)) for i in range(3)]
        ps_o = [es.enter_context(nc.psum_tensor(f"{name}_pso{i}", [128, 512], F32)) for i in range(2)]
        ps_d = [es.enter_context(nc.psum_tensor(f"{name}_psd{i}", [128, 512], F32)) for i in range(2)]
        t_c = P.op('dve', 'memset', ones[:], 1.0, inc=P.S('dve'))
        P.wait('pe', t_c)
        if dil:
            kraw_l = [es.enter_context(nc.sbuf_tensor(f"{name}_kraw{i}", [128, S], BF16)) for i in range(2)]
            kswp_l = [es.enter_context(nc.sbuf_tensor(f"{name}_kswp{i}", [128, S], BF16)) for i in range(2)]
            qraw_l = [es.enter_context(nc.sbuf_tensor(f"{name}_qraw{i}", [128, S], BF16)) for i in range(2)]
            qswp_l = [es.enter_context(nc.sbuf_tensor(f"{name}_qswp{i}", [128, S], BF16)) for i in range(2)]
            cosT = es.enter_context(nc.sbuf_tensor(f"{name}_cos", [128, S], F32))
            sinT = es.enter_context(nc.sbuf_tensor(f"{name}_sin", [128, S], F32))
            RC = min(1024, S)
            tmp1 = es.enter_context(nc.sbuf_tensor(f"{name}_tmp1", [128, RC], F32))
            tmp2 = es.enter_context(nc.sbuf_tensor(f"{name}_tmp2", [128, RC], F32))
            mask = es.enter_context(nc.sbuf_tensor(f"{name}_mask", [128, 20, 512], BF16))
            P.dma('sp', cosT[:], C.cos_ap, 'at_c')
            P.dma('sp', sinT[:], C.sin_ap, 'at_c')
            t_tab = P.dma('sp', mask[:], C.dmask_ap, 'at_c')
            P.wait('dve', t_tab)
        else:
            kb_l = [es.enter_context(nc.sbuf_tensor(f"{name}_kb{i}", [6, S], BF16)) for i in range(2)]
            qb_l = [es.enter_context(nc.sbuf_tensor(f"{name}_qb{i}", [6, S], BF16)) for i in range(2)]
            negm = es.enter_context(nc.sbuf_tensor(f"{name}_negm", [128, 4, 512], BF16))
            t_tab = P.dma('sp', negm[:], C.negm_ap, 'at_c')
            P.wait('pe', t_tab)
        pe_done = {}
        rope_done = {}
        ld_tick = {}
        ot_k = 0
        qblk = 0
        o_free = [None, None]
        t_exp = {}
        t_pv = {}
        gi = 0

        def load(h):
            bi = h % 2
            sem = f'at_ld{bi}'
            P.wait('sp', pe_done.get(h - 2))
            if dil:
                P.wait('sp', rope_done.get(h - 2))
                P.dma('sp', kraw_l[bi][:], KT[h], sem)
                P.dma('sp', qraw_l[bi][:], QT[h], sem)
                P.dma('sp', kswp_l[bi][0:64, :], KT[h, 64:128, :], sem)
                P.dma('sp', kswp_l[bi][64:128, :], KT[h, 0:64, :], sem)
                P.dma('sp', qswp_l[bi][0:64, :], QT[h, 64:128, :], sem)
                P.dma('sp', qswp_l[bi][64:128, :], QT[h, 0:64, :], sem)
            else:
                P.dma('sp', kt_l[bi][:], KT[h], sem)
                P.dma('sp', qt_l[bi][:], QT[h], sem)
                P.dma('sp', kb_l[bi][:], KB[h], sem)
                P.dma('sp', qb_l[bi][:], QB[h], sem)
            ld_tick[h] = P.dma('sp', v_l[bi][:], VVv[:, :, h * 128:(h + 1) * 128], sem)
        load(0)
        for h in range(H):
            bi = h % 2
            if h + 1 < H:
                load(h + 1)
            t_ld = ld_tick[h]
            v_ = v_l[bi]
            if dil:
                kt_, qt_ = kt_l[0], qt_l[0]
                kraw, kswp, qraw, qswp = kraw_l[bi], kswp_l[bi], qraw_l[bi], qswp_l[bi]
            else:
                kt_, qt_, kb_, qb_ = kt_l[bi], qt_l[bi], kb_l[bi], qb_l[bi]
            t_headfree = pe_done.get(h - 1)
            if dil:
                P.wait('dve', t_ld)
                P.wait('dve', t_headfree)
                for (raw, swp, dst) in ((kraw, kswp, kt_), (qraw, qswp, qt_)):
                    for r0 in range(0, S, RC):
                        sl = slice(r0, r0 + RC)
                        P.op('dve', 'tensor_tensor', out=tmp1[:], in0=raw[:, sl], in1=cosT[:, sl], op=ALU.mult)
                        P.op('dve', 'tensor_tensor', out=tmp2[:], in0=swp[:, sl], in1=sinT[:, sl], op=ALU.mult)
                        t_r = P.op('dve', 'tensor_tensor', out=dst[:, sl], in0=tmp1[:], in1=tmp2[:], op=ALU.add, inc=P.S('dve'))
                rope_done[h] = t_r
                P.wait('pe', t_r)
            else:
                P.wait('pe', t_ld)
            items = []
            for qb in range(NQB):
                k_lo = max(0, 4 * qb - 16) if dil else 0
                kts = list(range(k_lo, 4 * qb + 4))
                for j, kt in enumerate(kts):
                    items.append((qb, kt, j == 0, j == len(kts) - 1))

            def emit_qk(i):
                qb, kt, first, last = items[i]
                g = gi + i
                sb = ps_s[g % 3]
                P.wait('pe', t_exp.get(g - 3))
                qs = slice(qb * 512, (qb + 1) * 512)
                ks = slice(kt * 128, (kt + 1) * 128)
                extra = (not dil)
                diag = (not dil) and kt >= 4 * qb
                t = P.op('pe', 'matmul', sb[:], lhsT=kt_[:, ks], rhs=qt_[:, qs], start=True, stop=(not extra),
                         inc=(None if extra else P.S('pe')))
                if extra:
                    t = P.op('pe', 'matmul', sb[:], lhsT=kb_[0:6, ks], rhs=qb_[0:6, qs], start=False, stop=(not diag),
                             inc=(None if diag else P.S('pe')))
                    if diag:
                        t = P.op('pe', 'matmul', sb[:], lhsT=C.ident[:], rhs=negm[:, kt - 4 * qb, :], start=False, stop=True,
                                 inc=P.S('pe'))
                pb = pt[g % 4]
                P.wait('act', t)
                P.wait('act', t_pv.get(g - 4))
                te = P.op('act', 'activation', out=pb[:], in_=sb[:], func=AF.Exp,
                          scale=(float(C.HD ** -0.5) if dil else 1.0), inc=P.S('act'))
                t_exp[g] = te
                if dil:
                    P.wait('dve', te)
                    tm = P.op('dve', 'tensor_tensor', out=pb[:], in0=pb[:], in1=mask[:, 4 * qb - kt + 3, :], op=ALU.mult,
                              inc=P.S('dve'))
                    return tm
                return te

            t_ready = {}

            def emit_pv(i):
                nonlocal qblk, ot_k
                qb, kt, first, last = items[i]
                g = gi + i
                pb = pt[g % 4]
                oset = qblk % 2
                P.wait('pe', t_ready[i])
                if first:
                    P.wait('pe', o_free[oset])
                P.op('pe', 'matmul', ps_o[oset][:], lhsT=v_[:, kt, :], rhs=pb[:], start=first, stop=last)
                t = P.op('pe', 'matmul', ps_d[oset][:], lhsT=ones[:], rhs=pb[:], start=first, stop=last, inc=P.S('pe'))
                t_pv[g] = t
                if last:
                    ob = ot[ot_k % 2]
                    so = P.S(f'at_o{ot_k % 2}')
                    ot_k += 1
                    P.wait('dve', t)
                    P.wait('dve', (so, so.n))
                    P.op('dve', 'reciprocal', out=rden[:], in_=ps_d[oset][:])
                    tf = P.op('dve', 'tensor_tensor', out=ob[:], in0=ps_o[oset][:], in1=rden[:], op=ALU.mult, inc=P.S('dve'))
                    o_free[oset] = tf
                    P.wait('sp', tf)
                    P.dma('sp', OT[h, :, qb * 512:(qb + 1) * 512], ob[:], so, output=True)
                    qblk += 1
                return t

            n = len(items)
            DIST = 2
            t_last = None
            for i in range(n + DIST):
                if i < n:
                    t_ready[i] = emit_qk(i)
                if i - DIST >= 0:
                    t_last = emit_pv(i - DIST)
            gi += n
            pe_done[h] = t_last
        P.phase_end()


def phase_pool(P, C, OUT, PT, name):
    nc = C.nc
    S, D, DC = C.S, C.D, C.DC
    G = 4
    CG = D // G
    NCC = CG // 128
    NTB = S // 512
    OUTv = OUT.rearrange("(n p) d -> p n d", p=128)
    with ExitStack() as es:
        at = [es.enter_context(nc.sbuf_tensor(f"{name}_at{i}", [128, 5, 512], F32)) for i in range(2)]
        xs = [es.enter_context(nc.sbuf_tensor(f"{name}_xs{i}", [128, 5, CG], F32)) for i in range(2)]
        ob = [es.enter_context(nc.sbuf_tensor(f"{name}_ob{i}", [128, 512], BF16)) for i in range(3)]
        ps = [es.enter_context(nc.psum_tensor(f"{name}_ps{i}", [128, 512], F32)) for i in range(2)]
        x_free = [None, None]
        ps_free = [None, None]
        k = 0
        xi = 0
        t_atfree = None
        for g in range(G):
            P.wait('sp', t_atfree)
            P.dma('sp', at[0][:], C.pool_ap[0, g], 'pl_at')
            t_at = P.dma('sp', at[1][:], C.pool_ap[1, g], 'pl_at')
            P.wait('pe', t_at)
            for tb in range(NTB):
                b = xi % 2
                xi += 1
                P.wait('sp', x_free[b])
                r_lo = 0 if tb > 0 else 1
                n0 = tb * 4 - 1 + r_lo
                t_x = P.dma('sp', xs[b][:, r_lo:5, :], OUTv[:, n0:tb * 4 + 4, g * CG:(g + 1) * CG], f'pl_x{b}')
                P.wait('pe', t_x)
                A = at[0] if tb == 0 else at[1]
                for cc in range(NCC):
                    p = k % 2
                    P.wait('pe', ps_free[p])
                    for r in range(r_lo, 5):
                        t = P.op('pe', 'matmul', ps[p][:], lhsT=xs[b][:, r, cc * 128:(cc + 1) * 128], rhs=A[:, r, :],
                                 start=(r == r_lo), stop=(r == 4), inc=(P.S('pe') if r == 4 else None))
                    o = k % 3
                    so = P.S(f'pl_o{o}')
                    P.wait('act', t)
                    P.wait('act', (so, so.n))
                    te = P.op('act', 'activation', out=ob[o][:], in_=ps[p][:], func=AF.Copy, inc=P.S('act'))
                    ps_free[p] = te
                    P.wait('sp', te)
                    P.dma('sp', PT[g * NCC + cc, :, tb * 512:(tb + 1) * 512], ob[o][:], so, output=True)
                    k += 1
                x_free[b] = t
            t_atfree = t
        P.phase_end()


def epi_conv(P, C, UT, name):
    nc = C.nc
    DC = C.DC

    def epi(kind, *a):
        if kind == 'alloc':
            es = a[0]
            st = Ctx()
            NH = C.NH
            st.NH = NH
            st.bs = [es.enter_context(nc.sbuf_tensor(f"{name}_bs{i}", [128, 512], F32)) for i in range(NH)]
            st.cs = [es.enter_context(nc.sbuf_tensor(f"{name}_cs{i}", [128, 512], F32)) for i in range(NH)]
            st.z = es.enter_context(nc.sbuf_tensor(f"{name}_z", [128, 514], F32))
            st.acc = es.enter_context(nc.sbuf_tensor(f"{name}_acc", [128, 512], F32))
            st.halo = es.enter_context(nc.sbuf_tensor(f"{name}_halo", [128, DC, 2], F32))
            st.wc = es.enter_context(nc.sbuf_tensor(f"{name}_wc", [128, DC, 3], F32))
            st.ob = [es.enter_context(nc.sbuf_tensor(f"{name}_ob{i}", [128, 512], BF16)) for i in range(2)]
            st.k = 0
            t = P.dma('sp', st.wc[:], C.dconv_ap, 'cv_w')
            P.wait('dve', t)
            P.op('dve', 'memset', st.halo[:], 0.0)
            st.t_u = [None] * NH
            st.t_z = [None] * NH
            st.t_b = [None] * NH
            st.t_c = [None] * NH
            return st
        if kind in ('block', 'end'):
            return None
        tb, ci, ps, t_ready, st = a
        hf = tb % st.NH
        fc, which = ci // 3, ci % 3
        if which == 0:
            P.wait('act', t_ready)
            P.wait('act', st.t_u[hf])
            st.t_b[hf] = P.op('act', 'activation', out=st.bs[hf][:], in_=ps[:], func=AF.Copy, inc=P.S('act'))
            return st.t_b[hf]
        if which == 1:
            P.wait('act', t_ready)
            P.wait('act', st.t_z[hf])
            st.t_c[hf] = P.op('act', 'activation', out=st.cs[hf][:], in_=ps[:], func=AF.Copy, inc=P.S('act'))
            return st.t_c[hf]
        b = st.k % 2
        st.k += 1
        so = P.S(f'cv_o{b}')
        P.wait('dve', t_ready)
        P.wait('dve', st.t_c[hf])
        P.op('dve', 'tensor_copy', out=st.z[:, 0:2], in_=st.halo[:, fc, :])
        tz = P.op('dve', 'tensor_tensor', out=st.z[:, 2:514], in0=st.cs[hf][:], in1=ps[:], op=ALU.mult, inc=P.S('dve'))
        st.t_z[hf] = tz
        P.op('dve', 'tensor_copy', out=st.halo[:, fc, :], in_=st.z[:, 512:514])
        P.op('dve', 'tensor_scalar', out=st.acc[:], in0=st.z[:, 0:512], scalar1=st.wc[:, fc, 0:1], scalar2=None, op0=ALU.mult)
        P.op('dve', 'scalar_tensor_tensor', out=st.acc[:], in0=st.z[:, 1:513], scalar=st.wc[:, fc, 1:2], in1=st.acc[:],
             op0=ALU.mult, op1=ALU.add)
        P.op('dve', 'scalar_tensor_tensor', out=st.acc[:], in0=st.z[:, 2:514], scalar=st.wc[:, fc, 2:3], in1=st.acc[:],
             op0=ALU.mult, op1=ALU.add)
        P.wait('dve', st.t_b[hf])
        P.wait('dve', (so, so.n))
        tu = P.op('dve', 'tensor_tensor', out=st.ob[b][:], in0=st.acc[:], in1=st.bs[hf][:], op=ALU.mult, inc=P.S('dve'))
        st.t_u[hf] = tu
        P.wait('sp', tu)
        P.dma('sp', UT[fc, :, tb * 512:(tb + 1) * 512], st.ob[b][:], so, output=True)
        return tz
    return epi


def epi_fgate(P, C, QB, KB, name):
    nc = C.nc
    S, H = C.S, C.H

    def epi(kind, *a):
        if kind == 'alloc':
            es = a[0]
            st = Ctx()
            st.A = es.enter_context(nc.sbuf_tensor(f"{name}_A", [H, S], F32))
            st.B = es.enter_context(nc.sbuf_tensor(f"{name}_B", [H, S], F32))
            st.O = es.enter_context(nc.sbuf_tensor(f"{name}_O", [H, S], F32))
            st.e = es.enter_context(nc.sbuf_tensor(f"{name}_e", [H, 512], F32))
            st.bf = es.enter_context(nc.sbuf_tensor(f"{name}_bf", [H, 1], F32))
            st.h = [es.enter_context(nc.sbuf_tensor(f"{name}_h{i}", [H, S], BF16)) for i in range(7)]
            t = P.dma('sp', st.bf[:], C.cbf_ap, 'fg_b')
            P.wait('dve', t)
            P.op('dve', 'memset', st.O[:], 1.0)
            P.op('dve', 'memset', st.h[6][:], 1.0)
            t2 = P.op('dve', 'tensor_scalar', out=st.bf[:], in0=st.bf[:], scalar1=-1.0, scalar2=None, op0=ALU.mult, inc=P.S('dve'))
            P.wait('act', t2)
            st.t_last = None
            return st
        if kind == 'block':
            return None
        if kind == 'chunk':
            tb, ci, ps, t_ready, st = a
            P.wait('act', t_ready)
            P.wait('act', st.t_last)
            P.op('act', 'activation', out=st.e[:], in_=ps[0:H, :], func=AF.Exp, bias=st.bf[:, 0:1], scale=-1.0)
            te = P.op('act', 'activation', out=st.e[:], in_=st.e[:], func=AF.Ln, bias=1.0, scale=1.0, inc=P.S('act'))
            P.wait('dve', te)
            st.t_last = P.op('dve', 'tensor_scalar', out=st.A[:, tb * 512:(tb + 1) * 512], in0=st.e[:], scalar1=-1.0, scalar2=None,
                             op0=ALU.mult, inc=P.S('dve'))
            return te
        if kind == 'end':
            st = a[0]
            A, B, O, hh = st.A, st.B, st.O, st.h
            P.op('dve', 'tensor_tensor_scan', out=B[:], data0=O[:], data1=A[:], initial=0.0, op0=ALU.mult, op1=ALU.add)
            P.op('dve', 'tensor_copy', out=hh[0][:], in_=B[:])
            P.op('dve', 'tensor_copy', out=A[:], in_=hh[0][:])
            P.op('dve', 'tensor_tensor', out=A[:], in0=B[:], in1=A[:], op=ALU.subtract)
            P.op('dve', 'tensor_copy', out=hh[1][:], in_=A[:])
            P.op('dve', 'tensor_copy', out=B[:], in_=hh[1][:])
            P.op('dve', 'tensor_tensor', out=B[:], in0=A[:], in1=B[:], op=ALU.subtract)
            P.op('dve', 'tensor_copy', out=hh[2][:], in_=B[:])
            for i in range(3):
                t = P.op('dve', 'tensor_scalar', out=hh[3 + i][:], in0=hh[i][:], scalar1=-1.0, scalar2=None, op0=ALU.mult,
                         inc=P.S('dve'))
            P.wait('sp', t)
            for r in range(3):
                P.dma('sp', QB[:, r, :], hh[r][:], 'fg_o', output=True)
                P.dma('sp', QB[:, 3 + r, :], hh[6][:], 'fg_o', output=True)
                P.dma('sp', KB[:, r, :], hh[6][:], 'fg_o', output=True)
                P.dma('sp', KB[:, 3 + r, :], hh[3 + r][:], 'fg_o', output=True)
            return None
        return None
    return epi


CFG_FULL = dict(D=4096, S=4096, H=32, DFF=16384, B=4)


def host_consts(cfg):
    D, S, H = cfg['D'], cfg['S'], cfg['H']
    HD = D // H
    bf = ml_dtypes.bfloat16
    c = {}
    c['ident'] = np.eye(128, dtype=np.float32).astype(bf)
    pos = np.arange(S, dtype=np.float32)
    inv = (np.float32(10000.0) ** (-np.arange(0, HD, 2, dtype=np.float32) / np.float32(HD))).astype(np.float32)
    ang = (pos[:, None] * inv[None, :]).astype(np.float32)
    ang = np.concatenate([ang, ang], axis=-1)
    c['cosT'] = np.ascontiguousarray(np.cos(ang).astype(np.float32).T)
    sn = np.sin(ang).astype(np.float32).T.copy()
    sn[:HD // 2] *= -1.0
    c['sinT'] = np.ascontiguousarray(sn)
    k = np.arange(128)[:, None, None]
    idx = np.arange(20)[None, :, None]
    q = np.arange(512)[None, None, :]
    d = 128 * (idx - 3) + q - k
    m = ((d >= 0) & (d <= 128)).astype(np.float32) + ((d >= 0) & (d <= 512) & (d % 4 == 0)) + \
        ((d >= 0) & (d <= 2048) & (d % 16 == 0))
    c['dmask'] = m.astype(bf)
    j = np.arange(4)[None, :, None]
    c['negm'] = np.where(q - (128 * j + k) < 0, NEG, 0.0).astype(np.float32).astype(bf)
    at = np.zeros((2, 4, 128, 5, 512), np.float32)
    s = np.arange(128)[:, None, None]
    r = np.arange(5)[None, :, None]
    t = np.arange(512)[None, None, :]
    dd = t - ((r - 1) * 128 + s)
    for g, w in enumerate((2, 4, 8, 16)):
        inwin = (dd >= 0) & (dd < w)
        at[1, g] = inwin / np.float32(w) - (dd == 0)
        cnt = np.minimum(t + 1, w).astype(np.float32)
        a0 = inwin / cnt - (dd == 0)
        a0 = np.where(r == 0, 0.0, a0)
        at[0, g] = a0
    c['poolA'] = at
    return c


def build(cfg):
    D, S, H, DFF = cfg['D'], cfg['S'], cfg['H'], cfg['DFF']
    DC, FC = D // 128, DFF // 128
    nc = bass.Bass("TRN2", target_bir_lowering=False)
    C = Ctx()
    C.nc, C.D, C.S, C.H, C.HD, C.DC, C.FC = nc, D, S, H, D // H, DC, FC
    C.alpha = 8.0 ** 0.25

    def din(name, shape, dt=F32):
        return nc.dram_tensor(name, list(shape), dt, kind="ExternalInput")
    X = din("x", [S, D]).ap()
    C.h_lng = din("ln_g", [8, D]); C.h_lnb = din("ln_b", [8, D])
    W1 = din("mlp_w1", [4, D, DFF]).ap(); W2 = din("mlp_w2", [4, DFF, D]).ap()
    AQKV = din("a_wqkv", [D, 3 * D]).ap(); AWO = din("a_wo", [D, D]).ap()
    BW = din("b_wgrp", [D, D // 4]).ap(); C.h_bscale = din("b_scale", [1, D])
    CWIN = din("c_win", [D, 3 * D + H]).ap(); C.cbf_ap = din("c_bf", [H, 1]).ap(); CWO = din("c_wo", [D, D]).ap()
    DWIN = din("d_win", [D, 3 * D]).ap(); C.dconv_ap = din("d_conv", [128, DC, 3]).ap(); DWO = din("d_wout", [D, D]).ap()
    ident_d = din("ident", [128, 128], BF16).ap()
    C.cos_ap = din("cosT", [128, S]).ap(); C.sin_ap = din("sinT", [128, S]).ap()
    C.dmask_ap = din("dmask", [128, 20, 512], BF16).ap(); C.negm_ap = din("negm", [128, 4, 512], BF16).ap()
    C.pool_ap = din("poolA", [2, 4, 128, 5, 512]).ap()
    OUT = nc.dram_tensor("out", [S, D], F32, kind="ExternalOutput").ap()

    def scr(name, shape, dt=BF16):
        return nc.dram_tensor(name, list(shape), dt, kind="Internal").ap()
    XT = scr("XT", [DC, 128, S]); QT = scr("QT", [H, 128, S]); KT = scr("KT", [H, 128, S])
    VV = scr("VV", [S, D]); OT = scr("OT", [DC, 128, S]); HT = scr("HT", [FC, 128, S])
    QBd = scr("QBd", [H, 6, S]); KBd = scr("KBd", [H, 6, S])

    P = Prog(nc)
    C.caster = Caster(P)
    layers = cfg.get('layers', [0, 1, 2, 3])
    NWIN = 3 * D + H
    reg = {}

    def bcopy(name, W2d, rows, cols):
        dst = scr(name + "_bf", [rows, cols])
        reg[name] = len(reg)
        C.caster.add(reg[name], dst, W2d, rows, cols)
        return dst
    W1b, W2b = {}, {}
    if 0 in layers:
        AQKVb = bcopy("aqkv", AQKV, D, 3 * D); AWOb = bcopy("awo", AWO, D, D)
        W1b[0] = bcopy("w1_0", W1[0], D, DFF); W2b[0] = bcopy("w2_0", W2[0], DFF, D)
    if 1 in layers:
        BWb = bcopy("bw", BW, D, D // 4)
        W1b[1] = bcopy("w1_1", W1[1], D, DFF); W2b[1] = bcopy("w2_1", W2[1], DFF, D)
    if 2 in layers:
        CWINb = bcopy("cwin", CWIN, D, NWIN); CWOb = bcopy("cwo", CWO, D, D)
        W1b[2] = bcopy("w1_2", W1[2], D, DFF); W2b[2] = bcopy("w2_2", W2[2], DFF, D)
    if 3 in layers:
        DWINb = bcopy("dwin", DWIN, D, 3 * D); DWOb = bcopy("dwo", DWO, D, D)
        W1b[3] = bcopy("w1_3", W1[3], D, DFF); W2b[3] = bcopy("w2_3", W2[3], DFF, D)
    with ExitStack() as es:
        C.ident = es.enter_context(nc.sbuf_tensor("ident_sb", [128, 128], BF16))
        t = P.dma('sp', C.ident[:], ident_d, 'g_c')
        for e in ENG:
            P.wait(e, t)
        phase_prep(P, C, X, OUT, XT)

        def mlp(i):
            phase_lin_f(P, C, XT, DC, W1b[i], [f * 128 for f in range(FC)], epi_relu2(P, C, HT, f"m1_{i}"), f"m1_{i}",
                        region=reg[f"w1_{i}"])
            phase_lin_t_ks(P, C, HT, FC, W2b[i], D // 512,
                           epi_ln(P, C, OUT, XT, (2 * i + 1) * D, (2 * i + 1) * D, f"m2_{i}", 512), f"m2_{i}", 512,
                           region=reg[f"w2_{i}"])

        def wo_ln(i, W, src, rname):
            phase_lin_t(P, C, src, DC, W, D // 512, lambda cb: (list(range(DC)), list(range(DC)), cb * 512),
                        epi_ln(P, C, OUT, XT, (2 * i) * D, (2 * i) * D, f"wo_{i}", 512), f"wo_{i}", 512, region=reg[rname], NPS=6)

        def qkv(Win, qscale, tag, rname):
            phase_lin_f(P, C, XT, DC, Win, [h * 128 for h in range(H)], epi_copy_scale(P, C, QT, qscale, f"q{tag}"), f"q{tag}",
                        region=reg[rname])
            phase_lin_f(P, C, XT, DC, Win, [D + h * 128 for h in range(H)], epi_copy_scale(P, C, KT, 1.0, f"k{tag}"), f"k{tag}",
                        region=reg[rname])
            phase_lin_t(P, C, XT, DC, Win, D // 512, lambda cb: (list(range(DC)), list(range(DC)), 2 * D + cb * 512),
                        epi_v(P, C, VV, f"v{tag}"), f"v{tag}", 512, region=reg[rname], NPS=8)

        if 0 in layers:
            qkv(AQKVb, 1.0, "0", "aqkv")
            C.caster.emit(400, window=2)
            phase_attn(P, C, QT, KT, VV, OT, 'dil', "at0")
            wo_ln(0, AWOb, OT, "awo")
            mlp(0)
        if 1 in layers:
            C.caster.emit(64)
            phase_pool(P, C, OUT, OT, "pl")
            CG = D // 4
            NCC = CG // 128
            nper = CG // 512

            def kmap_pool(cb):
                g = cb // nper
                ks = [g * NCC + i for i in range(NCC)]
                return ks, ks, (cb % nper) * 512
            phase_lin_t(P, C, OT, DC, BWb, D // 512, kmap_pool,
                        epi_ln(P, C, OUT, XT, 2 * D, 2 * D, "pw", 256, scale_off=0), "pw", 256, region=reg["bw"])
            mlp(1)
        if 2 in layers:
            qkv(CWINb, float((D // H) ** -0.5), "2", "cwin")
            phase_lin_f(P, C, XT, DC, CWINb, [3 * D], epi_fgate(P, C, QBd, KBd, "fg"), "fg", M=H, TB=512, region=reg["cwin"], xbufs=1)
            C.caster.emit(400, window=2)
            phase_attn(P, C, QT, KT, VV, OT, 'fox', "at2", QB=QBd, KB=KBd)
            wo_ln(2, CWOb, OT, "cwo")
            mlp(2)
        if 3 in layers:
            chunks = []
            for fc in range(DC):
                chunks += [fc * 128, D + fc * 128, 2 * D + fc * 128]
            phase_lin_f(P, C, XT, DC, DWINb, chunks, epi_conv(P, C, OT, "cv"), "cv", region=reg["dwin"])
            wo_ln(3, DWOb, OT, "dwo")
            mlp(3)
        P.run()
    return nc


def make_in_maps(cfg, inputs):
    D, S, H = cfg['D'], cfg['S'], cfg['H']
    DC = D // 128
    B = cfg['B']
    c = host_consts(cfg)
    f32 = np.float32
    shared = {
        "ln_g": np.ascontiguousarray(inputs['ln_g'], f32).reshape(8, D),
        "ln_b": np.ascontiguousarray(inputs['ln_b'], f32).reshape(8, D),
        "mlp_w1": np.ascontiguousarray(inputs['mlp_w1'], f32),
        "mlp_w2": np.ascontiguousarray(inputs['mlp_w2'], f32),
        "a_wqkv": np.ascontiguousarray(inputs['a_wqkv'][0], f32),
        "a_wo": np.ascontiguousarray(inputs['a_wo'][0], f32),
        "b_wgrp": np.ascontiguousarray(inputs['b_wgrp'][0], f32).reshape(D, D // 4),
        "b_scale": np.ascontiguousarray(inputs['b_scale'], f32).reshape(1, D),
        "c_win": np.ascontiguousarray(inputs['c_win'][0], f32),
        "c_bf": np.ascontiguousarray(inputs['c_bf'], f32).reshape(H, 1),
        "c_wo": np.ascontiguousarray(inputs['c_wo'][0], f32),
        "d_win": np.ascontiguousarray(inputs['d_win'][0], f32),
        "d_conv": np.ascontiguousarray(np.asarray(inputs['d_conv'][0], f32).reshape(3, DC, 128).transpose(2, 1, 0)),
        "d_wout": np.ascontiguousarray(inputs['d_wout'][0], f32),
        "ident": c['ident'], "cosT": c['cosT'], "sinT": c['sinT'], "dmask": c['dmask'], "negm": c['negm'],
        "poolA": c['poolA'],
    }
    x = np.asarray(inputs['x'], f32)
    return [dict(shared, x=np.ascontiguousarray(x[b])) for b in range(B)]


def run_cfg(cfg, inputs):
    nc = build(cfg)
    in_maps = make_in_maps(cfg, inputs)
    res = run_bass_kernel_spmd(nc, in_maps, core_ids=list(range(cfg['B'])))
    return np.stack([res.results[b]["out"] for b in range(cfg['B'])], axis=0).astype(np.float32)


def kernel(**inputs):
    return run_cfg(CFG_FULL, inputs)
```

```python
import numpy as np
import ml_dtypes
from contextlib import ExitStack
import concourse.bass as bass
import concourse.mybir as mybir
from concourse.bass_utils import run_bass_kernel_spmd

F32 = mybir.dt.float32
BF16 = mybir.dt.bfloat16
AF = mybir.ActivationFunctionType
ALU = mybir.AluOpType
ENG = ['pe', 'act', 'dve', 'pool', 'sp']
LN_EPS = 1e-5
NEG = -30000.0


class Sem:
    def __init__(self, nc, name):
        self.h = nc.alloc_semaphore(name)
        self.n = 0


class Prog:
    def __init__(self, nc):
        self.nc = nc
        self.q = {e: [] for e in ENG}
        self.sems = {}
        self.pending = {}

    def S(self, name):
        if name not in self.sems:
            self.sems[name] = Sem(self.nc, name)
        return self.sems[name]

    def op(self, eng, meth, *a, inc=None, **kw):
        n = 0
        h = None
        t = None
        if eng in ('act', 'dve', 'pool'):
            inc = self.S(eng)
            if inc.n > 0:
                self.q[eng].append(('wait_ge', (inc.h, inc.n), {}, None, 0))
        if inc is not None:
            inc.n += 1
            h, n, t = inc.h, 1, (inc, inc.n)
        self.q[eng].append((meth, a, kw, h, n))
        return t

    def dma(self, eng, out, in_, sem, output=False):
        s = self.S(sem) if isinstance(sem, str) else sem
        s.n += 16
        self.q[eng].append(('dma_start', (), dict(out=out, in_=in_), s.h, 16))
        t = (s, s.n)
        if output:
            self.pending[s] = s.n
        return t

    def wait(self, eng, t):
        if t is None:
            return
        s, v = t
        if v <= 0:
            return
        self.q[eng].append(('wait_ge', (s.h, v), {}, None, 0))

    def phase_end(self):
        for s, v in self.pending.items():
            for e in ENG:
                self.wait(e, (s, v))
        self.pending = {}

    def run(self):
        nc = self.nc
        with nc.Block() as block:
            def mk(name):
                def f(e):
                    for meth, a, kw, h, n in self.q[name]:
                        ins = getattr(e, meth)(*a, **kw)
                        if h is not None:
                            ins.then_inc(h, n)
                return f
            block.tensor(mk('pe'))
            block.scalar(mk('act'))
            block.vector(mk('dve'))
            block.gpsimd(mk('pool'))
            block.sync(mk('sp'))


class Caster:
    NSEM = 8

    def __init__(self, P):
        self.P = P
        self.jobs = []
        self.next = 0
        self.tick = {}

    def add(self, region, dst, src, rows, cols):
        for r0 in range(0, rows, 128):
            for c0 in range(0, cols, 4096):
                c1 = min(cols, c0 + 4096)
                self.jobs.append((region, dst[r0:r0 + 128, c0:c1], src[r0:r0 + 128, c0:c1]))

    def emit(self, n, window=None):
        P = self.P
        while n > 0 and self.next < len(self.jobs):
            _, dst, src = self.jobs[self.next]
            s = P.S(f'cast{self.next % self.NSEM}')
            P.wait('pool', (s, s.n))
            if window is not None and self.next - window >= 0:
                P.wait('pool', self.tick.get(self.next - window))
            self.tick[self.next] = P.dma('pool', dst, src, s)
            self.next += 1
            n -= 1

    def ensure(self, region):
        P = self.P
        last = -1
        for i, j in enumerate(self.jobs):
            if j[0] <= region:
                last = i
        while self.next <= last:
            self.emit(1)
        for i in range(max(0, last - self.NSEM + 1), last + 1):
            P.wait('pool', self.tick[i])


def bcast_ap(handle, offset, n):
    return bass.AP(handle, offset, [[0, 128], [1, n]])


class Ctx:
    pass


def emit_transpose_store(P, C, rowb, t_rowb, XT, tt, stg, tps, k):
    DC = C.DC
    sb = stg[k % 2]
    s_st = P.S(f'ts_st{k % 2}')
    P.wait('act', (s_st, s_st.n))
    P.wait('pe', t_rowb)
    ng = (DC + 7) // 8
    t_last_pe = None
    for g in range(ng):
        gi = C.tp_cnt
        C.tp_cnt += 1
        pb = tps[gi % 2]
        P.wait('pe', C.tp_free[gi % 2])
        n = min(8, DC - g * 8)
        for j in range(n):
            c = g * 8 + j
            t = P.op('pe', 'transpose', out=pb[:, j * 128:(j + 1) * 128], in_=rowb[:, c * 128:(c + 1) * 128],
                     identity=C.ident[:], inc=(P.S('pe') if j == n - 1 else None))
        t_last_pe = t
        P.wait('act', t)
        tf = P.op('act', 'activation', out=sb[:, g * 8:g * 8 + n, :], in_=pb[:, 0:n * 128].rearrange("p (a b) -> p a b", b=128),
                  func=AF.Copy, inc=P.S('act'))
        C.tp_free[gi % 2] = tf
    P.wait('sp', tf)
    P.dma('sp', XT[:, :, tt * 128:(tt + 1) * 128].rearrange("c p t -> p c t"), sb[:], s_st, output=True)
    return t_last_pe


def phase_prep(P, C, X, OUT, XT):
    nc = C.nc
    D, S, DC = C.D, C.S, C.DC
    with ExitStack() as es:
        rows = [es.enter_context(nc.sbuf_tensor(f"pr_row{i}", [128, D], F32)) for i in range(2)]
        rowb = [es.enter_context(nc.sbuf_tensor(f"pr_rowb{i}", [128, D], BF16)) for i in range(2)]
        stg = [es.enter_context(nc.sbuf_tensor(f"pr_stg{i}", [128, DC, 128], BF16)) for i in range(2)]
        tps = [es.enter_context(nc.psum_tensor(f"pr_tps{i}", [128, 1024], BF16)) for i in range(2)]
        C.tp_cnt = 0
        C.tp_free = [None, None]
        free_row = [None, None]
        free_rowb = [None, None]
        for tt in range(S // 128):
            b = tt % 2
            P.wait('sp', free_row[b])
            t_ld = P.dma('sp', rows[b][:], X[tt * 128:(tt + 1) * 128, :], f'pr_ld{b}')
            P.wait('sp', t_ld)
            P.dma('sp', OUT[tt * 128:(tt + 1) * 128, :], rows[b][:], f'pr_so{b}', output=True)
            P.wait('dve', t_ld)
            P.wait('dve', free_rowb[b])
            t_c = P.op('dve', 'tensor_copy', out=rowb[b][:], in_=rows[b][:], inc=P.S('dve'))
            sso = P.S(f'pr_so{b}')
            free_row[b] = (sso, sso.n)
            free_rowb[b] = emit_transpose_store(P, C, rowb[b], t_c, XT, tt, stg, tps, tt)
        P.phase_end()


def phase_lin_f(P, C, XTin, KC, W, col_chunks, epi, name, M=128, TB=1024, region=None, xbufs=2):
    nc = C.nc
    S = C.S
    TB = min(TB, S)
    NTB = S // TB
    NH = TB // 512
    Wv = W.rearrange("(kc p) f -> p kc f", p=128)
    groups = []
    i = 0
    while i < len(col_chunks):
        if M == 128 and i + 1 < len(col_chunks) and col_chunks[i + 1] == col_chunks[i] + 128:
            groups.append((col_chunks[i], 2, i))
            i += 2
        else:
            groups.append((col_chunks[i], 1, i))
            i += 1
    if region is not None:
        C.caster.ensure(region)
    with ExitStack() as es:
        xbs = [es.enter_context(nc.sbuf_tensor(f"{name}_xb{i}", [128, KC, TB], BF16)) for i in range(xbufs)]
        NW = 3
        wt = [es.enter_context(nc.sbuf_tensor(f"{name}_wt{i}", [128, KC, 256], BF16)) for i in range(NW)]
        NPS = 4
        ps = [es.enter_context(nc.psum_tensor(f"{name}_ps{i}", [128, 512], F32)) for i in range(NPS)]
        C.NH = NH
        st = epi('alloc', es)
        ps_free = [None] * NPS
        w_free = [None] * NW
        wi = 0
        pi = 0
        x_done = [None] * xbufs
        x_tick = {}

        def xload(tb):
            bi = tb % xbufs
            P.wait('sp', x_done[bi])
            x_tick[tb] = P.dma('sp', xbs[bi][:], XTin[:, :, tb * TB:(tb + 1) * TB].rearrange("k p t -> p k t"), f'g_x{bi}')
        if xbufs > 1:
            xload(0)
        for tb in range(NTB):
            if xbufs > 1:
                if tb + 1 < NTB:
                    xload(tb + 1)
            else:
                xload(tb)
            xb = xbs[tb % xbufs]
            epi('block', tb, st)
            P.wait('pe', x_tick[tb])
            for (c0, nch, ci0) in groups:
                b = wi % NW
                wi += 1
                C.caster.emit(1)
                P.wait('pool', w_free[b])
                t_w = P.dma('pool', wt[b][:, :, 0:nch * M], Wv[:, :, c0:c0 + nch * M], f'g_w{b}')
                P.wait('pe', t_w)
                for m in range(nch):
                    for hf in range(NH):
                        p = pi % NPS
                        pi += 1
                        P.wait('pe', ps_free[p])
                        for k in range(KC):
                            t = P.op('pe', 'matmul', ps[p][0:M, :], lhsT=wt[b][:, k, m * M:(m + 1) * M],
                                     rhs=xb[:, k, hf * 512:(hf + 1) * 512],
                                     start=(k == 0), stop=(k == KC - 1), inc=(P.S('pe') if k == KC - 1 else None))
                        ps_free[p] = epi('chunk', tb * NH + hf, ci0 + m, ps[p], t, st)
                w_free[b] = t
            x_done[tb % xbufs] = t
        epi('end', st)
        P.phase_end()


def epi_copy_scale(P, C, OUTT, scale, name):
    nc = C.nc

    def epi(kind, *a):
        if kind == 'alloc':
            es = a[0]
            st = Ctx()
            st.ob = [es.enter_context(nc.sbuf_tensor(f"{name}_ob{i}", [128, 512], BF16)) for i in range(3)]
            st.k = 0
            return st
        if kind in ('block', 'end'):
            return None
        tb, ci, ps, t_ready, st = a
        b = st.k % 3
        st.k += 1
        so = P.S(f'g_o{b}')
        P.wait('act', t_ready)
        P.wait('act', (so, so.n))
        t = P.op('act', 'activation', out=st.ob[b][:], in_=ps[:], func=AF.Copy, scale=float(scale), inc=P.S('act'))
        P.wait('sp', t)
        P.dma('sp', OUTT[ci, :, tb * 512:(tb + 1) * 512], st.ob[b][:], so, output=True)
        return t
    return epi


def epi_relu2(P, C, OUTT, name):
    nc = C.nc

    def epi(kind, *a):
        if kind == 'alloc':
            es = a[0]
            st = Ctx()
            st.r = [es.enter_context(nc.sbuf_tensor(f"{name}_r{i}", [128, 512], F32)) for i in range(2)]
            st.ob = [es.enter_context(nc.sbuf_tensor(f"{name}_ob{i}", [128, 512], BF16)) for i in range(3)]
            st.k = 0
            st.rfree = [None, None]
            return st
        if kind in ('block', 'end'):
            return None
        tb, ci, ps, t_ready, st = a
        b = st.k % 3
        rb = st.k % 2
        st.k += 1
        so = P.S(f'g_o{b}')
        P.wait('act', t_ready)
        P.wait('act', st.rfree[rb])
        t = P.op('act', 'activation', out=st.r[rb][:], in_=ps[:], func=AF.Relu, inc=P.S('act'))
        P.wait('dve', t)
        P.wait('dve', (so, so.n))
        t2 = P.op('dve', 'tensor_tensor', out=st.ob[b][:], in0=st.r[rb][:], in1=st.r[rb][:], op=ALU.mult, inc=P.S('dve'))
        st.rfree[rb] = t2
        P.wait('sp', t2)
        P.dma('sp', OUTT[ci, :, tb * 512:(tb + 1) * 512], st.ob[b][:], so, output=True)
        return t
    return epi


def phase_lin_t(P, C, XTin, KC, W, ncb, kmap, epi, name, TB, region=None, NPS=6):
    nc = C.nc
    S = C.S
    NTB = S // TB
    NT = TB // 128
    Wv = W.rearrange("(kc p) f -> p kc f", p=128)
    KG = 16
    if region is not None:
        C.caster.ensure(region)
    with ExitStack() as es:
        xb = es.enter_context(nc.sbuf_tensor(f"{name}_xb", [128, KC, TB], BF16))
        NW = 2
        wt = [es.enter_context(nc.sbuf_tensor(f"{name}_wt{i}", [128, KG, 512], BF16)) for i in range(NW)]
        ps = [es.enter_context(nc.psum_tensor(f"{name}_ps{i}", [128, 512], F32)) for i in range(NPS)]
        st = epi('alloc', es)
        ps_free = [None] * NPS
        w_free = [None] * NW
        wi = 0
        pi = 0
        t_xfree = None
        for tb in range(NTB):
            P.wait('sp', t_xfree)
            t_x = P.dma('sp', xb[:], XTin[:, :, tb * TB:(tb + 1) * TB].rearrange("k p t -> p k t"), 'g_x')
            epi('block', tb, st)
            P.wait('pe', t_x)
            for cb in range(ncb):
                xks, wks, wc0 = kmap(cb)
                nk = len(xks)
                ngr = (nk + KG - 1) // KG
                pbase = pi
                pi += NT
                for g in range(ngr):
                    b = wi % NW
                    wi += 1
                    k0 = g * KG
                    n = min(KG, nk - k0)
                    C.caster.emit(1)
                    P.wait('pool', w_free[b])
                    t_w = P.dma('pool', wt[b][:, 0:n, :], Wv[:, wks[k0]:wks[k0] + n, wc0:wc0 + 512], f'g_w{b}')
                    P.wait('pe', t_w)
                    for tt in range(NT):
                        p = (pbase + tt) % NPS
                        if g == 0:
                            P.wait('pe', ps_free[p])
                        for k in range(n):
                            kk = k0 + k
                            last = (kk == nk - 1)
                            t = P.op('pe', 'matmul', ps[p][:], lhsT=xb[:, xks[kk], tt * 128:(tt + 1) * 128], rhs=wt[b][:, k, :],
                                     start=(kk == 0), stop=last,
                                     inc=(P.S('pe') if (k == n - 1) else None))
                        if g == ngr - 1:
                            ps_free[p] = epi('tile', tb, tt, cb, ps[p], t, st)
                    w_free[b] = t
            t_xfree = t
            epi('blockend', tb, st)
        epi('end', st)
        P.phase_end()


def phase_lin_t_ks(P, C, XTin, KC, W, ncb, epi, name, TB, region=None, NPS=6):
    nc = C.nc
    S = C.S
    NTB = S // TB
    NT = TB // 128
    PK = 16
    nparts = KC // PK
    Wv = W.rearrange("(kc p) f -> p kc f", p=128)
    if region is not None:
        C.caster.ensure(region)
    with ExitStack() as es:
        xb = [es.enter_context(nc.sbuf_tensor(f"{name}_xb{i}", [128, PK, TB], BF16)) for i in range(2)]
        NW = 3
        wt = [es.enter_context(nc.sbuf_tensor(f"{name}_wt{i}", [128, PK, 512], BF16)) for i in range(NW)]
        ps = [es.enter_context(nc.psum_tensor(f"{name}_ps{i}", [128, 512], F32)) for i in range(NPS)]
        st = epi('alloc', es)
        ps_free = [None] * NPS
        w_free = [None] * NW
        wi = 0
        pi = 0
        seq = [(tb, part) for tb in range(NTB) for part in range(nparts)]
        x_done = [None, None]
        x_tick = {}

        def load(idx):
            tb, part = seq[idx]
            bi = idx % 2
            P.wait('sp', x_done[bi])
            x_tick[idx] = P.dma('sp', xb[bi][:], XTin[part * PK:(part + 1) * PK, :, tb * TB:(tb + 1) * TB].rearrange("k p t -> p k t"),
                                f'g_xk{bi}')
        load(0)
        for idx, (tb, part) in enumerate(seq):
            bi = idx % 2
            if idx + 1 < len(seq):
                load(idx + 1)
            if part == 0:
                epi('block', tb, st)
            P.wait('pe', x_tick[idx])
            for cb in range(ncb):
                b = wi % NW
                wi += 1
                if wi % 2 == 0:
                    C.caster.emit(1)
                P.wait('pool', w_free[b])
                t_w = P.dma('pool', wt[b][:], Wv[:, part * PK:(part + 1) * PK, cb * 512:(cb + 1) * 512], f'g_w{b}')
                P.wait('pe', t_w)
                for tt in range(NT):
                    p = pi % NPS
                    pi += 1
                    P.wait('pe', ps_free[p])
                    for k in range(PK):
                        t = P.op('pe', 'matmul', ps[p][:], lhsT=xb[bi][:, k, tt * 128:(tt + 1) * 128], rhs=wt[b][:, k, :],
                                 start=(k == 0), stop=(k == PK - 1), inc=(P.S('pe') if k == PK - 1 else None))
                    ps_free[p] = epi('tile', tb, tt, cb, ps[p], t, st, part == 0, part == nparts - 1)
                w_free[b] = t
            x_done[bi] = t
            if part == nparts - 1:
                epi('blockend', tb, st)
        epi('end', st)
        P.phase_end()


def epi_v(P, C, VV, name):
    nc = C.nc

    def epi(kind, *a):
        if kind == 'alloc':
            es = a[0]
            st = Ctx()
            st.ob = [es.enter_context(nc.sbuf_tensor(f"{name}_ob{i}", [128, 512], BF16)) for i in range(3)]
            st.k = 0
            return st
        if kind != 'tile':
            return None
        tb, tt, cb, ps, t_ready, st = a
        b = st.k % 3
        st.k += 1
        so = P.S(f'g_o{b}')
        P.wait('act', t_ready)
        P.wait('act', (so, so.n))
        t = P.op('act', 'activation', out=st.ob[b][:], in_=ps[:], func=AF.Copy, inc=P.S('act'))
        P.wait('sp', t)
        r0 = tb * 512 + tt * 128
        P.dma('sp', VV[r0:r0 + 128, cb * 512:(cb + 1) * 512], st.ob[b][:], so, output=True)
        return t
    return epi


def epi_ln(P, C, OUT, XT, g_off, b_off, name, TB, scale_off=None):
    nc = C.nc
    D, DC = C.D, C.DC
    NT = TB // 128
    ncb = D // 512

    def epi(kind, *a):
        if kind == 'alloc':
            es = a[0]
            st = Ctx()
            st.rows = [es.enter_context(nc.sbuf_tensor(f"{name}_row{i}", [128, D], F32)) for i in range(NT)]
            st.rowb = es.enter_context(nc.sbuf_tensor(f"{name}_rowb", [128, D], BF16))
            st.gB = es.enter_context(nc.sbuf_tensor(f"{name}_gB", [128, D], F32))
            st.bB = es.enter_context(nc.sbuf_tensor(f"{name}_bB", [128, D], F32))
            st.stat = es.enter_context(nc.sbuf_tensor(f"{name}_stat", [128, 8], F32))
            st.stg = [es.enter_context(nc.sbuf_tensor(f"{name}_stg{i}", [128, DC, 128], BF16)) for i in range(2)]
            st.tps = [es.enter_context(nc.psum_tensor(f"{name}_tps{i}", [128, 1024], BF16)) for i in range(2)]
            st.bst = es.enter_context(nc.sbuf_tensor(f"{name}_bst", [128, D // 512, 6], F32))
            C.tp_cnt = 0
            C.tp_free = [None, None]
            st.tk = 0
            t1 = P.dma('sp', st.gB[:], bcast_ap(C.h_lng, g_off, D), 'g_gb')
            t2 = P.dma('sp', st.bB[:], bcast_ap(C.h_lnb, b_off, D), 'g_gb')
            st.t_gb = t2
            if scale_off is not None:
                st.sB = es.enter_context(nc.sbuf_tensor(f"{name}_sB", [128, D], F32))
                st.tmp = es.enter_context(nc.sbuf_tensor(f"{name}_tmp", [128, 512], F32))
                st.t_gb = P.dma('sp', st.sB[:], bcast_ap(C.h_bscale, scale_off, D), 'g_gb')
            P.wait('dve', st.t_gb)
            st.row_free = [None] * NT
            st.rowb_free = None
            st.t_rows = [None] * NT
            return st
        if kind == 'block':
            tb, st = a
            for tt in range(NT):
                if st.row_free[tt] is not None:
                    P.wait('sp', st.row_free[tt][0])
                    P.wait('sp', st.row_free[tt][1])
                r0 = tb * TB + tt * 128
                st.t_rows[tt] = P.dma('sp', st.rows[tt][:], OUT[r0:r0 + 128, :], f'g_ldrow{tt}')
            return None
        if kind == 'tile':
            if len(a) == 8:
                tb, tt, cb, ps, t_ready, st, kfirst, klast = a
            else:
                tb, tt, cb, ps, t_ready, st = a
                kfirst, klast = True, True
            row = st.rows[tt]
            P.wait('dve', t_ready)
            if cb == 0 and kfirst:
                P.wait('dve', st.t_rows[tt])
            sl = slice(cb * 512, (cb + 1) * 512)
            if not kfirst:
                t = P.op('dve', 'tensor_tensor', out=row[:, sl], in0=row[:, sl], in1=ps[:], op=ALU.add, inc=P.S('dve'))
            elif scale_off is not None:
                P.op('dve', 'tensor_tensor', out=st.tmp[:], in0=ps[:], in1=st.sB[:, sl], op=ALU.mult)
                t = P.op('dve', 'scalar_tensor_tensor', out=row[:, sl], in0=row[:, sl], scalar=float(C.alpha), in1=st.tmp[:],
                         op0=ALU.mult, op1=ALU.add, inc=P.S('dve'))
            else:
                t = P.op('dve', 'scalar_tensor_tensor', out=row[:, sl], in0=row[:, sl], scalar=float(C.alpha), in1=ps[:],
                         op0=ALU.mult, op1=ALU.add, inc=P.S('dve'))
            if cb == ncb - 1 and klast:
                stat = st.stat
                for c8 in range(D // 512):
                    P.op('dve', 'bn_stats', out=st.bst[:, c8, :], in_=row[:, c8 * 512:(c8 + 1) * 512])
                P.op('dve', 'bn_aggr', out=stat[:, 2:4], in_=st.bst[:].rearrange("p a b -> p (a b)"))
                tv = P.op('dve', 'tensor_scalar', out=stat[:, 4:5], in0=stat[:, 3:4], scalar1=float(LN_EPS), scalar2=None,
                          op0=ALU.add)
                P.wait('act', tv)
                tq = P.op('act', 'activation', out=stat[:, 4:5], in_=stat[:, 4:5], func=AF.Sqrt)
                P.wait('dve', tq)
                P.op('dve', 'reciprocal', out=stat[:, 5:6], in_=stat[:, 4:5])
                P.op('dve', 'tensor_tensor', out=stat[:, 6:7], in0=stat[:, 2:3], in1=stat[:, 5:6], op=ALU.mult)
                tb_ = P.op('dve', 'tensor_scalar', out=stat[:, 6:7], in0=stat[:, 6:7], scalar1=-1.0, scalar2=None, op0=ALU.mult,
                           inc=P.S('dve'))
                P.wait('act', tb_)
                tn = P.op('act', 'activation', out=row[:], in_=row[:], func=AF.Identity, bias=stat[:, 6:7], scale=stat[:, 5:6],
                          inc=P.S('act'))
                P.wait('dve', tn)
                P.op('dve', 'tensor_tensor', out=row[:], in0=row[:], in1=st.gB[:], op=ALU.mult)
                tg = P.op('dve', 'tensor_tensor', out=row[:], in0=row[:], in1=st.bB[:], op=ALU.add, inc=P.S('dve'))
                P.wait('sp', tg)
                r0 = tb * TB + tt * 128
                tso = P.dma('sp', OUT[r0:r0 + 128, :], row[:], f'g_strow{tt}', output=True)
                P.wait('act', tg)
                P.wait('act', st.rowb_free)
                tc_ = P.op('act', 'activation', out=st.rowb[:], in_=row[:], func=AF.Copy, inc=P.S('act'))
                st.rowb_free = emit_transpose_store(P, C, st.rowb, tc_, XT, (tb * TB) // 128 + tt, st.stg, st.tps, st.tk)
                st.tk += 1
                st.row_free[tt] = (tso, tc_)
            return t
        return None
    return epi


def phase_attn(P, C, QT, KT, VV, OT, kind, name, QB=None, KB=None):
    nc = C.nc
    S, H = C.S, C.H
    NQB = S // 512
    NKT = S // 128
    dil = (kind == 'dil')
    VVv = VV.rearrange("(n p) d -> p n d", p=128)
    with ExitStack() as es:
        nkq = 1 if dil else 2
        kt_l = [es.enter_context(nc.sbuf_tensor(f"{name}_kt{i}", [128, S], BF16)) for i in range(nkq)]
        qt_l = [es.enter_context(nc.sbuf_tensor(f"{name}_qt{i}", [128, S], BF16)) for i in range(nkq)]
        v_l = [es.enter_context(nc.sbuf_tensor(f"{name}_v{i}", [128, NKT, 128], BF16)) for i in range(2)]
        pt = [es.enter_context(nc.sbuf_tensor(f"{name}_pt{i}", [128, 512], BF16)) for i in range(5)]
        ot = [es.enter_context(nc.sbuf_tensor(f"{name}_ot{i}", [128, 512], BF16)) for i in range(2)]
        rden = es.enter_context(nc.sbuf_tensor(f"{name}_rden", [128, 512], F32))
        ones = es.enter_context(nc.sbuf_tensor(f"{name}_ones", [128, 128], BF16))
        ps_s = [es.enter_context(nc.psum_tensor(f"{name}_pss{i}", [128, 512], F32)) for i in range(4)]
        ps_o = [es.enter_context(nc.psum_tensor(f"{name}_pso{i}", [128, 512], F32)) for i in range(2)]
        ps_d = [es.enter_context(nc.psum_tensor(f"{name}_psd{i}", [128, 512], F32)) for i in range(2)]
        t_c = P.op('dve', 'memset', ones[:], 1.0, inc=P.S('dve'))
        P.wait('pe', t_c)
        if dil:
            kraw_l = [es.enter_context(nc.sbuf_tensor(f"{name}_kraw{i}", [128, S], BF16)) for i in range(2)]
            kswp_l = [es.enter_context(nc.sbuf_tensor(f"{name}_kswp{i}", [128, S], BF16)) for i in range(2)]
            qraw_l = [es.enter_context(nc.sbuf_tensor(f"{name}_qraw{i}", [128, S], BF16)) for i in range(2)]
            qswp_l = [es.enter_context(nc.sbuf_tensor(f"{name}_qswp{i}", [128, S], BF16)) for i in range(2)]
            cosT = es.enter_context(nc.sbuf_tensor(f"{name}_cos", [128, S], F32))
            sinT = es.enter_context(nc.sbuf_tensor(f"{name}_sin", [128, S], F32))
            RC = min(1024, S)
            tmp1 = es.enter_context(nc.sbuf_tensor(f"{name}_tmp1", [128, RC], F32))
            tmp2 = es.enter_context(nc.sbuf_tensor(f"{name}_tmp2", [128, RC], F32))
            mask = es.enter_context(nc.sbuf_tensor(f"{name}_mask", [128, 20, 512], BF16))
            P.dma('sp', cosT[:], C.cos_ap, 'at_c')
            P.dma('sp', sinT[:], C.sin_ap, 'at_c')
            t_tab = P.dma('sp', mask[:], C.dmask_ap, 'at_c')
            P.wait('dve', t_tab)
        else:
            kb_l = [es.enter_context(nc.sbuf_tensor(f"{name}_kb{i}", [6, S], BF16)) for i in range(2)]
            qb_l = [es.enter_context(nc.sbuf_tensor(f"{name}_qb{i}", [6, S], BF16)) for i in range(2)]
            negm = es.enter_context(nc.sbuf_tensor(f"{name}_negm", [128, 4, 512], BF16))
            t_tab = P.dma('sp', negm[:], C.negm_ap, 'at_c')
            P.wait('pe', t_tab)
        pe_done = {}
        rope_done = {}
        ld_tick = {}
        ot_k = 0
        qblk = 0
        o_free = [None, None]
        t_exp = {}
        t_pv = {}
        gi = 0

        def load(h):
            bi = h % 2
            sem = f'at_ld{bi}'
            P.wait('sp', pe_done.get(h - 2))
            if dil:
                P.wait('sp', rope_done.get(h - 2))
                P.dma('sp', kraw_l[bi][:], KT[h], sem)
                P.dma('sp', qraw_l[bi][:], QT[h], sem)
                P.dma('sp', kswp_l[bi][0:64, :], KT[h, 64:128, :], sem)
                P.dma('sp', kswp_l[bi][64:128, :], KT[h, 0:64, :], sem)
                P.dma('sp', qswp_l[bi][0:64, :], QT[h, 64:128, :], sem)
                P.dma('sp', qswp_l[bi][64:128, :], QT[h, 0:64, :], sem)
            else:
                P.dma('sp', kt_l[bi][:], KT[h], sem)
                P.dma('sp', qt_l[bi][:], QT[h], sem)
                P.dma('sp', kb_l[bi][:], KB[h], sem)
                P.dma('sp', qb_l[bi][:], QB[h], sem)
            ld_tick[h] = P.dma('sp', v_l[bi][:], VVv[:, :, h * 128:(h + 1) * 128], sem)
        load(0)
        for h in range(H):
            bi = h % 2
            if h + 1 < H:
                load(h + 1)
            t_ld = ld_tick[h]
            v_ = v_l[bi]
            if dil:
                kt_, qt_ = kt_l[0], qt_l[0]
                kraw, kswp, qraw, qswp = kraw_l[bi], kswp_l[bi], qraw_l[bi], qswp_l[bi]
            else:
                kt_, qt_, kb_, qb_ = kt_l[bi], qt_l[bi], kb_l[bi], qb_l[bi]
            t_headfree = pe_done.get(h - 1)
            if dil:
                P.wait('dve', t_ld)
                P.wait('dve', t_headfree)
                for (raw, swp, dst) in ((kraw, kswp, kt_), (qraw, qswp, qt_)):
                    for r0 in range(0, S, RC):
                        sl = slice(r0, r0 + RC)
                        P.op('dve', 'tensor_tensor', out=tmp1[:], in0=raw[:, sl], in1=cosT[:, sl], op=ALU.mult)
                        P.op('dve', 'tensor_tensor', out=tmp2[:], in0=swp[:, sl], in1=sinT[:, sl], op=ALU.mult)
                        t_r = P.op('dve', 'tensor_tensor', out=dst[:, sl], in0=tmp1[:], in1=tmp2[:], op=ALU.add, inc=P.S('dve'))
                rope_done[h] = t_r
                P.wait('pe', t_r)
            else:
                P.wait('pe', t_ld)
            items = []
            for qb in range(NQB):
                k_lo = max(0, 4 * qb - 16) if dil else 0
                kts = list(range(k_lo, 4 * qb + 4))
                for j, kt in enumerate(kts):
                    items.append((qb, kt, j == 0, j == len(kts) - 1))

            def emit_qk(i):
                qb, kt, first, last = items[i]
                g = gi + i
                sb = ps_s[g % 4]
                P.wait('pe', t_exp.get(g - 4))
                qs = slice(qb * 512, (qb + 1) * 512)
                ks = slice(kt * 128, (kt + 1) * 128)
                extra = (not dil)
                diag = (not dil) and kt >= 4 * qb
                t = P.op('pe', 'matmul', sb[:], lhsT=kt_[:, ks], rhs=qt_[:, qs], start=True, stop=(not extra),
                         inc=(None if extra else P.S('pe')))
                if extra:
                    t = P.op('pe', 'matmul', sb[:], lhsT=kb_[0:6, ks], rhs=qb_[0:6, qs], start=False, stop=(not diag),
                             inc=(None if diag else P.S('pe')))
                    if diag:
                        t = P.op('pe', 'matmul', sb[:], lhsT=C.ident[:], rhs=negm[:, kt - 4 * qb, :], start=False, stop=True,
                                 inc=P.S('pe'))
                pb = pt[g % 5]
                P.wait('act', t)
                P.wait('act', t_pv.get(g - 5))
                te = P.op('act', 'activation', out=pb[:], in_=sb[:], func=AF.Exp,
                          scale=(float(C.HD ** -0.5) if dil else 1.0), inc=P.S('act'))
                t_exp[g] = te
                if dil:
                    P.wait('dve', te)
                    tm = P.op('dve', 'tensor_tensor', out=pb[:], in0=pb[:], in1=mask[:, 4 * qb - kt + 3, :], op=ALU.mult,
                              inc=P.S('dve'))
                    return tm
                return te

            t_ready = {}

            def emit_pv(i):
                nonlocal qblk, ot_k
                qb, kt, first, last = items[i]
                g = gi + i
                pb = pt[g % 5]
                oset = qblk % 2
                P.wait('pe', t_ready[i])
                if first:
                    P.wait('pe', o_free[oset])
                P.op('pe', 'matmul', ps_o[oset][:], lhsT=v_[:, kt, :], rhs=pb[:], start=first, stop=last)
                t = P.op('pe', 'matmul', ps_d[oset][:], lhsT=ones[:], rhs=pb[:], start=first, stop=last, inc=P.S('pe'))
                t_pv[g] = t
                if last:
                    ob = ot[ot_k % 2]
                    so = P.S(f'at_o{ot_k % 2}')
                    ot_k += 1
                    P.wait('dve', t)
                    P.wait('dve', (so, so.n))
                    P.op('dve', 'reciprocal', out=rden[:], in_=ps_d[oset][:])
                    tf = P.op('dve', 'tensor_tensor', out=ob[:], in0=ps_o[oset][:], in1=rden[:], op=ALU.mult, inc=P.S('dve'))
                    o_free[oset] = tf
                    P.wait('sp', tf)
                    P.dma('sp', OT[h, :, qb * 512:(qb + 1) * 512], ob[:], so, output=True)
                    qblk += 1
                return t

            n = len(items)
            DIST = 3
            t_last = None
            for i in range(n + DIST):
                if i < n:
                    t_ready[i] = emit_qk(i)
                if i - DIST >= 0:
                    t_last = emit_pv(i - DIST)
            gi += n
            pe_done[h] = t_last
        P.phase_end()


def phase_pool(P, C, OUT, PT, name):
    nc = C.nc
    S, D, DC = C.S, C.D, C.DC
    G = 4
    CG = D // G
    NCC = CG // 128
    NTB = S // 512
    OUTv = OUT.rearrange("(n p) d -> p n d", p=128)
    with ExitStack() as es:
        at = [es.enter_context(nc.sbuf_tensor(f"{name}_at{i}", [128, 5, 512], F32)) for i in range(2)]
        xs = [es.enter_context(nc.sbuf_tensor(f"{name}_xs{i}", [128, 5, CG], F32)) for i in range(2)]
        ob = [es.enter_context(nc.sbuf_tensor(f"{name}_ob{i}", [128, 512], BF16)) for i in range(3)]
        ps = [es.enter_context(nc.psum_tensor(f"{name}_ps{i}", [128, 512], F32)) for i in range(2)]
        x_free = [None, None]
        ps_free = [None, None]
        k = 0
        xi = 0
        t_atfree = None
        for g in range(G):
            P.wait('sp', t_atfree)
            P.dma('sp', at[0][:], C.pool_ap[0, g], 'pl_at')
            t_at = P.dma('sp', at[1][:], C.pool_ap[1, g], 'pl_at')
            P.wait('pe', t_at)
            for tb in range(NTB):
                b = xi % 2
                xi += 1
                P.wait('sp', x_free[b])
                r_lo = 0 if tb > 0 else 1
                n0 = tb * 4 - 1 + r_lo
                t_x = P.dma('sp', xs[b][:, r_lo:5, :], OUTv[:, n0:tb * 4 + 4, g * CG:(g + 1) * CG], f'pl_x{b}')
                P.wait('pe', t_x)
                A = at[0] if tb == 0 else at[1]
                for cc in range(NCC):
                    p = k % 2
                    P.wait('pe', ps_free[p])
                    for r in range(r_lo, 5):
                        t = P.op('pe', 'matmul', ps[p][:], lhsT=xs[b][:, r, cc * 128:(cc + 1) * 128], rhs=A[:, r, :],
                                 start=(r == r_lo), stop=(r == 4), inc=(P.S('pe') if r == 4 else None))
                    o = k % 3
                    so = P.S(f'pl_o{o}')
                    P.wait('act', t)
                    P.wait('act', (so, so.n))
                    te = P.op('act', 'activation', out=ob[o][:], in_=ps[p][:], func=AF.Copy, inc=P.S('act'))
                    ps_free[p] = te
                    P.wait('sp', te)
                    P.dma('sp', PT[g * NCC + cc, :, tb * 512:(tb + 1) * 512], ob[o][:], so, output=True)
                    k += 1
                x_free[b] = t
            t_atfree = t
        P.phase_end()


def epi_conv(P, C, UT, name):
    nc = C.nc
    DC = C.DC

    def epi(kind, *a):
        if kind == 'alloc':
            es = a[0]
            st = Ctx()
            NH = C.NH
            st.NH = NH
            st.bs = [es.enter_context(nc.sbuf_tensor(f"{name}_bs{i}", [128, 512], F32)) for i in range(NH)]
            st.cs = [es.enter_context(nc.sbuf_tensor(f"{name}_cs{i}", [128, 512], F32)) for i in range(NH)]
            st.z = es.enter_context(nc.sbuf_tensor(f"{name}_z", [128, 514], F32))
            st.acc = es.enter_context(nc.sbuf_tensor(f"{name}_acc", [128, 512], F32))
            st.halo = es.enter_context(nc.sbuf_tensor(f"{name}_halo", [128, DC, 2], F32))
            st.wc = es.enter_context(nc.sbuf_tensor(f"{name}_wc", [128, DC, 3], F32))
            st.ob = [es.enter_context(nc.sbuf_tensor(f"{name}_ob{i}", [128, 512], BF16)) for i in range(2)]
            st.k = 0
            t = P.dma('sp', st.wc[:], C.dconv_ap, 'cv_w')
            P.wait('dve', t)
            P.op('dve', 'memset', st.halo[:], 0.0)
            st.t_u = [None] * NH
            st.t_z = [None] * NH
            st.t_b = [None] * NH
            st.t_c = [None] * NH
            return st
        if kind in ('block', 'end'):
            return None
        tb, ci, ps, t_ready, st = a
        hf = tb % st.NH
        fc, which = ci // 3, ci % 3
        if which == 0:
            P.wait('act', t_ready)
            P.wait('act', st.t_u[hf])
            st.t_b[hf] = P.op('act', 'activation', out=st.bs[hf][:], in_=ps[:], func=AF.Copy, inc=P.S('act'))
            return st.t_b[hf]
        if which == 1:
            P.wait('act', t_ready)
            P.wait('act', st.t_z[hf])
            st.t_c[hf] = P.op('act', 'activation', out=st.cs[hf][:], in_=ps[:], func=AF.Copy, inc=P.S('act'))
            return st.t_c[hf]
        b = st.k % 2
        st.k += 1
        so = P.S(f'cv_o{b}')
        P.wait('dve', t_ready)
        P.wait('dve', st.t_c[hf])
        P.op('dve', 'tensor_copy', out=st.z[:, 0:2], in_=st.halo[:, fc, :])
        tz = P.op('dve', 'tensor_tensor', out=st.z[:, 2:514], in0=st.cs[hf][:], in1=ps[:], op=ALU.mult, inc=P.S('dve'))
        st.t_z[hf] = tz
        P.op('dve', 'tensor_copy', out=st.halo[:, fc, :], in_=st.z[:, 512:514])
        P.op('dve', 'tensor_scalar', out=st.acc[:], in0=st.z[:, 0:512], scalar1=st.wc[:, fc, 0:1], scalar2=None, op0=ALU.mult)
        P.op('dve', 'scalar_tensor_tensor', out=st.acc[:], in0=st.z[:, 1:513], scalar=st.wc[:, fc, 1:2], in1=st.acc[:],
             op0=ALU.mult, op1=ALU.add)
        P.op('dve', 'scalar_tensor_tensor', out=st.acc[:], in0=st.z[:, 2:514], scalar=st.wc[:, fc, 2:3], in1=st.acc[:],
             op0=ALU.mult, op1=ALU.add)
        P.wait('dve', st.t_b[hf])
        P.wait('dve', (so, so.n))
        tu = P.op('dve', 'tensor_tensor', out=st.ob[b][:], in0=st.acc[:], in1=st.bs[hf][:], op=ALU.mult, inc=P.S('dve'))
        st.t_u[hf] = tu
        P.wait('sp', tu)
        P.dma('sp', UT[fc, :, tb * 512:(tb + 1) * 512], st.ob[b][:], so, output=True)
        return tz
    return epi


def epi_fgate(P, C, QB, KB, name):
    nc = C.nc
    S, H = C.S, C.H

    def epi(kind, *a):
        if kind == 'alloc':
            es = a[0]
            st = Ctx()
            st.A = es.enter_context(nc.sbuf_tensor(f"{name}_A", [H, S], F32))
            st.B = es.enter_context(nc.sbuf_tensor(f"{name}_B", [H, S], F32))
            st.O = es.enter_context(nc.sbuf_tensor(f"{name}_O", [H, S], F32))
            st.e = es.enter_context(nc.sbuf_tensor(f"{name}_e", [H, 512], F32))
            st.bf = es.enter_context(nc.sbuf_tensor(f"{name}_bf", [H, 1], F32))
            st.h = [es.enter_context(nc.sbuf_tensor(f"{name}_h{i}", [H, S], BF16)) for i in range(7)]
            t = P.dma('sp', st.bf[:], C.cbf_ap, 'fg_b')
            P.wait('dve', t)
            P.op('dve', 'memset', st.O[:], 1.0)
            P.op('dve', 'memset', st.h[6][:], 1.0)
            t2 = P.op('dve', 'tensor_scalar', out=st.bf[:], in0=st.bf[:], scalar1=-1.0, scalar2=None, op0=ALU.mult, inc=P.S('dve'))
            P.wait('act', t2)
            st.t_last = None
            return st
        if kind == 'block':
            return None
        if kind == 'chunk':
            tb, ci, ps, t_ready, st = a
            P.wait('act', t_ready)
            P.wait('act', st.t_last)
            P.op('act', 'activation', out=st.e[:], in_=ps[0:H, :], func=AF.Exp, bias=st.bf[:, 0:1], scale=-1.0)
            te = P.op('act', 'activation', out=st.e[:], in_=st.e[:], func=AF.Ln, bias=1.0, scale=1.0, inc=P.S('act'))
            P.wait('dve', te)
            st.t_last = P.op('dve', 'tensor_scalar', out=st.A[:, tb * 512:(tb + 1) * 512], in0=st.e[:], scalar1=-1.0, scalar2=None,
                             op0=ALU.mult, inc=P.S('dve'))
            return te
        if kind == 'end':
            st = a[0]
            A, B, O, hh = st.A, st.B, st.O, st.h
            P.op('dve', 'tensor_tensor_scan', out=B[:], data0=O[:], data1=A[:], initial=0.0, op0=ALU.mult, op1=ALU.add)
            P.op('dve', 'tensor_copy', out=hh[0][:], in_=B[:])
            P.op('dve', 'tensor_copy', out=A[:], in_=hh[0][:])
            P.op('dve', 'tensor_tensor', out=A[:], in0=B[:], in1=A[:], op=ALU.subtract)
            P.op('dve', 'tensor_copy', out=hh[1][:], in_=A[:])
            P.op('dve', 'tensor_copy', out=B[:], in_=hh[1][:])
            P.op('dve', 'tensor_tensor', out=B[:], in0=A[:], in1=B[:], op=ALU.subtract)
            P.op('dve', 'tensor_copy', out=hh[2][:], in_=B[:])
            for i in range(3):
                t = P.op('dve', 'tensor_scalar', out=hh[3 + i][:], in0=hh[i][:], scalar1=-1.0, scalar2=None, op0=ALU.mult,
                         inc=P.S('dve'))
            P.wait('sp', t)
            for r in range(3):
                P.dma('sp', QB[:, r, :], hh[r][:], 'fg_o', output=True)
                P.dma('sp', QB[:, 3 + r, :], hh[6][:], 'fg_o', output=True)
                P.dma('sp', KB[:, r, :], hh[6][:], 'fg_o', output=True)
                P.dma('sp', KB[:, 3 + r, :], hh[3 + r][:], 'fg_o', output=True)
            return None
        return None
    return epi


CFG_FULL = dict(D=4096, S=4096, H=32, DFF=16384, B=4)


def host_consts(cfg):
    D, S, H = cfg['D'], cfg['S'], cfg['H']
    HD = D // H
    bf = ml_dtypes.bfloat16
    c = {}
    c['ident'] = np.eye(128, dtype=np.float32).astype(bf)
    pos = np.arange(S, dtype=np.float32)
    inv = (np.float32(10000.0) ** (-np.arange(0, HD, 2, dtype=np.float32) / np.float32(HD))).astype(np.float32)
    ang = (pos[:, None] * inv[None, :]).astype(np.float32)
    ang = np.concatenate([ang, ang], axis=-1)
    c['cosT'] = np.ascontiguousarray(np.cos(ang).astype(np.float32).T)
    sn = np.sin(ang).astype(np.float32).T.copy()
    sn[:HD // 2] *= -1.0
    c['sinT'] = np.ascontiguousarray(sn)
    k = np.arange(128)[:, None, None]
    idx = np.arange(20)[None, :, None]
    q = np.arange(512)[None, None, :]
    d = 128 * (idx - 3) + q - k
    m = ((d >= 0) & (d <= 128)).astype(np.float32) + ((d >= 0) & (d <= 512) & (d % 4 == 0)) + \
        ((d >= 0) & (d <= 2048) & (d % 16 == 0))
    c['dmask'] = m.astype(bf)
    j = np.arange(4)[None, :, None]
    c['negm'] = np.where(q - (128 * j + k) < 0, NEG, 0.0).astype(np.float32).astype(bf)
    at = np.zeros((2, 4, 128, 5, 512), np.float32)
    s = np.arange(128)[:, None, None]
    r = np.arange(5)[None, :, None]
    t = np.arange(512)[None, None, :]
    dd = t - ((r - 1) * 128 + s)
    for g, w in enumerate((2, 4, 8, 16)):
        inwin = (dd >= 0) & (dd < w)
        at[1, g] = inwin / np.float32(w) - (dd == 0)
        cnt = np.minimum(t + 1, w).astype(np.float32)
        a0 = inwin / cnt - (dd == 0)
        a0 = np.where(r == 0, 0.0, a0)
        at[0, g] = a0
    c['poolA'] = at
    return c


def build(cfg):
    D, S, H, DFF = cfg['D'], cfg['S'], cfg['H'], cfg['DFF']
    DC, FC = D // 128, DFF // 128
    nc = bass.Bass("TRN2", target_bir_lowering=False)
    C = Ctx()
    C.nc, C.D, C.S, C.H, C.HD, C.DC, C.FC = nc, D, S, H, D // H, DC, FC
    C.alpha = 8.0 ** 0.25

    def din(name, shape, dt=F32):
        return nc.dram_tensor(name, list(shape), dt, kind="ExternalInput")
    X = din("x", [S, D]).ap()
    C.h_lng = din("ln_g", [8, D]); C.h_lnb = din("ln_b", [8, D])
    W1 = din("mlp_w1", [4, D, DFF]).ap(); W2 = din("mlp_w2", [4, DFF, D]).ap()
    AQKV = din("a_wqkv", [D, 3 * D]).ap(); AWO = din("a_wo", [D, D]).ap()
    BW = din("b_wgrp", [D, D // 4]).ap(); C.h_bscale = din("b_scale", [1, D])
    CWIN = din("c_win", [D, 3 * D + H]).ap(); C.cbf_ap = din("c_bf", [H, 1]).ap(); CWO = din("c_wo", [D, D]).ap()
    DWIN = din("d_win", [D, 3 * D]).ap(); C.dconv_ap = din("d_conv", [128, DC, 3]).ap(); DWO = din("d_wout", [D, D]).ap()
    ident_d = din("ident", [128, 128], BF16).ap()
    C.cos_ap = din("cosT", [128, S]).ap(); C.sin_ap = din("sinT", [128, S]).ap()
    C.dmask_ap = din("dmask", [128, 20, 512], BF16).ap(); C.negm_ap = din("negm", [128, 4, 512], BF16).ap()
    C.pool_ap = din("poolA", [2, 4, 128, 5, 512]).ap()
    OUT = nc.dram_tensor("out", [S, D], F32, kind="ExternalOutput").ap()

    def scr(name, shape, dt=BF16):
        return nc.dram_tensor(name, list(shape), dt, kind="Internal").ap()
    XT = scr("XT", [DC, 128, S]); QT = scr("QT", [H, 128, S]); KT = scr("KT", [H, 128, S])
    VV = scr("VV", [S, D]); OT = scr("OT", [DC, 128, S]); HT = scr("HT", [FC, 128, S])
    QBd = scr("QBd", [H, 6, S]); KBd = scr("KBd", [H, 6, S])

    P = Prog(nc)
    C.caster = Caster(P)
    layers = cfg.get('layers', [0, 1, 2, 3])
    NWIN = 3 * D + H
    reg = {}

    def bcopy(name, W2d, rows, cols):
        dst = scr(name + "_bf", [rows, cols])
        reg[name] = len(reg)
        C.caster.add(reg[name], dst, W2d, rows, cols)
        return dst
    W1b, W2b = {}, {}
    if 0 in layers:
        AQKVb = bcopy("aqkv", AQKV, D, 3 * D); AWOb = bcopy("awo", AWO, D, D)
        W1b[0] = bcopy("w1_0", W1[0], D, DFF); W2b[0] = bcopy("w2_0", W2[0], DFF, D)
    if 1 in layers:
        BWb = bcopy("bw", BW, D, D // 4)
        W1b[1] = bcopy("w1_1", W1[1], D, DFF); W2b[1] = bcopy("w2_1", W2[1], DFF, D)
    if 2 in layers:
        CWINb = bcopy("cwin", CWIN, D, NWIN); CWOb = bcopy("cwo", CWO, D, D)
        W1b[2] = bcopy("w1_2", W1[2], D, DFF); W2b[2] = bcopy("w2_2", W2[2], DFF, D)
    if 3 in layers:
        DWINb = bcopy("dwin", DWIN, D, 3 * D); DWOb = bcopy("dwo", DWO, D, D)
        W1b[3] = bcopy("w1_3", W1[3], D, DFF); W2b[3] = bcopy("w2_3", W2[3], DFF, D)
    with ExitStack() as es:
        C.ident = es.enter_context(nc.sbuf_tensor("ident_sb", [128, 128], BF16))
        t = P.dma('sp', C.ident[:], ident_d, 'g_c')
        for e in ENG:
            P.wait(e, t)
        phase_prep(P, C, X, OUT, XT)

        def mlp(i):
            phase_lin_f(P, C, XT, DC, W1b[i], [f * 128 for f in range(FC)], epi_relu2(P, C, HT, f"m1_{i}"), f"m1_{i}",
                        region=reg[f"w1_{i}"])
            phase_lin_t_ks(P, C, HT, FC, W2b[i], D // 512,
                           epi_ln(P, C, OUT, XT, (2 * i + 1) * D, (2 * i + 1) * D, f"m2_{i}", 512), f"m2_{i}", 512,
                           region=reg[f"w2_{i}"])

        def wo_ln(i, W, src, rname):
            phase_lin_t(P, C, src, DC, W, D // 512, lambda cb: (list(range(DC)), list(range(DC)), cb * 512),
                        epi_ln(P, C, OUT, XT, (2 * i) * D, (2 * i) * D, f"wo_{i}", 512), f"wo_{i}", 512, region=reg[rname], NPS=6)

        def qkv(Win, qscale, tag, rname):
            phase_lin_f(P, C, XT, DC, Win, [h * 128 for h in range(H)], epi_copy_scale(P, C, QT, qscale, f"q{tag}"), f"q{tag}",
                        region=reg[rname])
            phase_lin_f(P, C, XT, DC, Win, [D + h * 128 for h in range(H)], epi_copy_scale(P, C, KT, 1.0, f"k{tag}"), f"k{tag}",
                        region=reg[rname])
            phase_lin_t(P, C, XT, DC, Win, D // 512, lambda cb: (list(range(DC)), list(range(DC)), 2 * D + cb * 512),
                        epi_v(P, C, VV, f"v{tag}"), f"v{tag}", 512, region=reg[rname], NPS=8)

        if 0 in layers:
            qkv(AQKVb, 1.0, "0", "aqkv")
            C.caster.emit(400, window=2)
            phase_attn(P, C, QT, KT, VV, OT, 'dil', "at0")
            wo_ln(0, AWOb, OT, "awo")
            mlp(0)
        if 1 in layers:
            C.caster.emit(64)
            phase_pool(P, C, OUT, OT, "pl")
            CG = D // 4
            NCC = CG // 128
            nper = CG // 512

            def kmap_pool(cb):
                g = cb // nper
                ks = [g * NCC + i for i in range(NCC)]
                return ks, ks, (cb % nper) * 512
            phase_lin_t(P, C, OT, DC, BWb, D // 512, kmap_pool,
                        epi_ln(P, C, OUT, XT, 2 * D, 2 * D, "pw", 256, scale_off=0), "pw", 256, region=reg["bw"])
            mlp(1)
        if 2 in layers:
            qkv(CWINb, float((D // H) ** -0.5), "2", "cwin")
            phase_lin_f(P, C, XT, DC, CWINb, [3 * D], epi_fgate(P, C, QBd, KBd, "fg"), "fg", M=H, TB=512, region=reg["cwin"], xbufs=1)
            C.caster.emit(400, window=2)
            phase_attn(P, C, QT, KT, VV, OT, 'fox', "at2", QB=QBd, KB=KBd)
            wo_ln(2, CWOb, OT, "cwo")
            mlp(2)
        if 3 in layers:
            chunks = []
            for fc in range(DC):
                chunks += [fc * 128, D + fc * 128, 2 * D + fc * 128]
            phase_lin_f(P, C, XT, DC, DWINb, chunks, epi_conv(P, C, OT, "cv"), "cv", region=reg["dwin"])
            wo_ln(3, DWOb, OT, "dwo")
            mlp(3)
        P.run()
    return nc


def make_in_maps(cfg, inputs):
    D, S, H = cfg['D'], cfg['S'], cfg['H']
    DC = D // 128
    B = cfg['B']
    c = host_consts(cfg)
    f32 = np.float32
    shared = {
        "ln_g": np.ascontiguousarray(inputs['ln_g'], f32).reshape(8, D),
        "ln_b": np.ascontiguousarray(inputs['ln_b'], f32).reshape(8, D),
        "mlp_w1": np.ascontiguousarray(inputs['mlp_w1'], f32),
        "mlp_w2": np.ascontiguousarray(inputs['mlp_w2'], f32),
        "a_wqkv": np.ascontiguousarray(inputs['a_wqkv'][0], f32),
        "a_wo": np.ascontiguousarray(inputs['a_wo'][0], f32),
        "b_wgrp": np.ascontiguousarray(inputs['b_wgrp'][0], f32).reshape(D, D // 4),
        "b_scale": np.ascontiguousarray(inputs['b_scale'], f32).reshape(1, D),
        "c_win": np.ascontiguousarray(inputs['c_win'][0], f32),
        "c_bf": np.ascontiguousarray(inputs['c_bf'], f32).reshape(H, 1),
        "c_wo": np.ascontiguousarray(inputs['c_wo'][0], f32),
        "d_win": np.ascontiguousarray(inputs['d_win'][0], f32),
        "d_conv": np.ascontiguousarray(np.asarray(inputs['d_conv'][0], f32).reshape(3, DC, 128).transpose(2, 1, 0)),
        "d_wout": np.ascontiguousarray(inputs['d_wout'][0], f32),
        "ident": c['ident'], "cosT": c['cosT'], "sinT": c['sinT'], "dmask": c['dmask'], "negm": c['negm'],
        "poolA": c['poolA'],
    }
    x = np.asarray(inputs['x'], f32)
    return [dict(shared, x=np.ascontiguousarray(x[b])) for b in range(B)]


def run_cfg(cfg, inputs):
    nc = build(cfg)
    in_maps = make_in_maps(cfg, inputs)
    res = run_bass_kernel_spmd(nc, in_maps, core_ids=list(range(cfg['B'])))
    return np.stack([res.results[b]["out"] for b in range(cfg['B'])], axis=0).astype(np.float32)


def kernel(**inputs):
    return run_cfg(CFG_FULL, inputs)
```
